# Optimizing a Trainium2 kernel written in Bass

```python
import math
import jax, jax.numpy as jnp
from jax import lax
import numpy as np

D_MODEL = 1024
BATCH = 4
SEQ = 4096
DEPTH = 1
DEC_BATCH = 32
DEC_SEQ = 32
PAST_LEN = 2048

CHUNK = 64
D_CONV = 1024
CONV_WIDTH = 3
D_SSM = 1024
GROUP_SIZE = 16
N_GROUPS = D_SSM // GROUP_SIZE
STATE_DIM = 64
D_FF = 2816
N_IN = 3 * D_CONV + D_SSM + 2 * D_MODEL
RMS_EPS = 1e-6
DT_MIN = 1e-3
DT_MAX = 1e-1

kernel_name = "hybrid_shortconv_s5_macaron_step"


def rms_norm(x, g):
    xf = x.astype(jnp.float32)
    y = xf * lax.rsqrt(jnp.mean(xf * xf, axis=-1, keepdims=True) + RMS_EPS)
    return (y * g.astype(jnp.float32)).astype(x.dtype)


def swiglu(h, wg, wu, wd):
    return (jax.nn.silu(h @ wg) * (h @ wu)) @ wd


def short_conv(z, prev, w):
    L = z.shape[1]
    zp = jnp.concatenate([prev.astype(z.dtype), z], axis=1)
    out = w[0] * zp[:, 0:L]
    for k in range(1, CONV_WIDTH):
        out = out + w[k] * zp[:, k:k + L]
    return out, zp[:, L:]


def s5_scan(u, h0, lam_re, lam_im, log_step, b_re, b_im, c_re, c_im, d_skip):
    f32 = jnp.float32
    bsz, L, _ = u.shape
    uf = u.astype(f32)
    ug = uf.reshape(bsz, L, N_GROUPS, GROUP_SIZE)
    lam = lax.complex(lam_re.astype(f32), lam_im.astype(f32))
    step = jnp.exp(log_step.astype(f32))[:, None]
    lam_bar = jnp.exp(lam * step)
    b_bar = ((lam_bar - 1.0) / lam)[..., None] * lax.complex(b_re.astype(f32), b_im.astype(f32))
    bu = lax.complex(jnp.einsum('gpc,blgc->blgp', jnp.real(b_bar), ug),
                     jnp.einsum('gpc,blgc->blgp', jnp.imag(b_bar), ug))
    a = jnp.broadcast_to(lam_bar, bu.shape)

    def combine(left, right):
        a1, b1 = left
        a2, b2 = right
        return a2 * a1, a2 * b1 + b2

    a_cum, h = lax.associative_scan(combine, (a, bu), axis=1)
    if h0 is not None:
        h = h + a_cum * lax.complex(h0[0].astype(f32), h0[1].astype(f32))[:, None]
    c = lax.complex(c_re.astype(f32), c_im.astype(f32))
    y = jnp.real(jnp.einsum('gcp,blgp->blgc', c, h)).reshape(bsz, L, D_SSM)
    y = y + d_skip.astype(f32) * uf
    h_last = h[:, -1]
    return y.astype(u.dtype), jnp.real(h_last).astype(u.dtype), jnp.imag(h_last).astype(u.dtype)


def mixer(h, conv_prev, h0, w_in, w_conv, w_conv_out, lam_re, lam_im, log_step,
          b_re, b_im, c_re, c_im, d_skip, w_glu, w_o):
    proj = h @ w_in
    v, c_gate, b_gate, u, g_conv, g_ssm = jnp.split(
        proj, [D_CONV, 2 * D_CONV, 3 * D_CONV, 3 * D_CONV + D_SSM, 3 * D_CONV + D_SSM + D_MODEL], axis=-1)
    conv_out, conv_state = short_conv(c_gate * v, conv_prev, w_conv)
    y_conv = (b_gate * conv_out) @ w_conv_out
    y_s, s_re, s_im = s5_scan(u, h0, lam_re, lam_im, log_step, b_re, b_im, c_re, c_im, d_skip)
    glu = jax.nn.gelu(y_s) @ w_glu
    y_ssm = glu[..., :D_MODEL] * jax.nn.sigmoid(glu[..., D_MODEL:])
    merged = jax.nn.sigmoid(g_conv) * y_conv + jax.nn.sigmoid(g_ssm) * y_ssm
    return merged @ w_o, conv_state, s_re, s_im


def trunk(x, conv_prev, ssm_re0, ssm_im0, params):
    (norm_ffn1, w_ffn1_gate, w_ffn1_up, w_ffn1_down, norm_mix, w_in, w_conv, w_conv_out,
     ssm_lambda_re, ssm_lambda_im, ssm_log_step, ssm_b_re, ssm_b_im, ssm_c_re, ssm_c_im, ssm_d,
     w_glu, w_o, norm_ffn2, w_ffn2_gate, w_ffn2_up, w_ffn2_down, norm_final) = params
    convs, res, ims = [], [], []
    for l in range(DEPTH):
        x = x + 0.5 * swiglu(rms_norm(x, norm_ffn1[l]), w_ffn1_gate[l], w_ffn1_up[l], w_ffn1_down[l])
        if conv_prev is None:
            prev = jnp.zeros((x.shape[0], CONV_WIDTH - 1, D_CONV), x.dtype)
            h0 = None
        else:
            prev = conv_prev[l]
            h0 = (ssm_re0[l], ssm_im0[l])
        mix, cs, sre, sim = mixer(rms_norm(x, norm_mix[l]), prev, h0, w_in[l], w_conv[l], w_conv_out[l],
                                  ssm_lambda_re[l], ssm_lambda_im[l], ssm_log_step[l], ssm_b_re[l], ssm_b_im[l],
                                  ssm_c_re[l], ssm_c_im[l], ssm_d[l], w_glu[l], w_o[l])
        x = x + mix
        x = x + 0.5 * swiglu(rms_norm(x, norm_ffn2[l]), w_ffn2_gate[l], w_ffn2_up[l], w_ffn2_down[l])
        convs.append(cs)
        res.append(sre)
        ims.append(sim)
    return rms_norm(x, norm_final), jnp.stack(convs), jnp.stack(res), jnp.stack(ims)


def setup_inputs(seed: int = 0) -> dict:
    key = jax.random.key(seed)
    k = jax.random.split(key, 32)
    f32 = jnp.float32

    def nrm(i, shape, scale):
        return jax.random.normal(k[i], shape, f32) * scale

    def gain(i, shape):
        return 1.0 + 0.01 * jax.random.normal(k[i], shape, f32)

    lam_im = jnp.broadcast_to(math.pi * jnp.arange(STATE_DIM, dtype=f32), (DEPTH, N_GROUPS, STATE_DIM))
    return {
        "x_prompt": nrm(0, (BATCH, SEQ, D_MODEL), 1.0),
        "x_sample": nrm(1, (DEC_BATCH, DEC_SEQ, D_MODEL), 1.0),
        "state_conv": nrm(2, (DEPTH, DEC_BATCH, CONV_WIDTH - 1, D_CONV), 1.0),
        "state_ssm_re": nrm(3, (DEPTH, DEC_BATCH, N_GROUPS, STATE_DIM), 0.1),
        "state_ssm_im": nrm(4, (DEPTH, DEC_BATCH, N_GROUPS, STATE_DIM), 0.1),
        "norm_ffn1": gain(5, (DEPTH, D_MODEL)),
        "w_ffn1_gate": nrm(6, (DEPTH, D_MODEL, D_FF), D_MODEL ** -0.5),
        "w_ffn1_up": nrm(7, (DEPTH, D_MODEL, D_FF), D_MODEL ** -0.5),
        "w_ffn1_down": nrm(8, (DEPTH, D_FF, D_MODEL), D_FF ** -0.5),
        "norm_mix": gain(9, (DEPTH, D_MODEL)),
        "w_in": nrm(10, (DEPTH, D_MODEL, N_IN), D_MODEL ** -0.5),
        "w_conv": nrm(11, (DEPTH, CONV_WIDTH, D_CONV), CONV_WIDTH ** -0.5),
        "w_conv_out": nrm(12, (DEPTH, D_CONV, D_MODEL), D_CONV ** -0.5),
        "ssm_lambda_re": -0.5 + nrm(13, (DEPTH, N_GROUPS, STATE_DIM), 0.01),
        "ssm_lambda_im": lam_im + nrm(14, (DEPTH, N_GROUPS, STATE_DIM), 0.01),
        "ssm_log_step": jax.random.uniform(k[15], (DEPTH, N_GROUPS), f32,
                                           minval=math.log(DT_MIN), maxval=math.log(DT_MAX)),
        "ssm_b_re": nrm(16, (DEPTH, N_GROUPS, STATE_DIM, GROUP_SIZE), (2 * GROUP_SIZE) ** -0.5),
        "ssm_b_im": nrm(17, (DEPTH, N_GROUPS, STATE_DIM, GROUP_SIZE), (2 * GROUP_SIZE) ** -0.5),
        "ssm_c_re": nrm(18, (DEPTH, N_GROUPS, GROUP_SIZE, STATE_DIM), STATE_DIM ** -0.5),
        "ssm_c_im": nrm(19, (DEPTH, N_GROUPS, GROUP_SIZE, STATE_DIM), STATE_DIM ** -0.5),
        "ssm_d": nrm(20, (DEPTH, D_SSM), 1.0),
        "w_glu": nrm(21, (DEPTH, D_SSM, 2 * D_MODEL), D_SSM ** -0.5),
        "w_o": nrm(22, (DEPTH, D_MODEL, D_MODEL), D_MODEL ** -0.5),
        "norm_ffn2": gain(23, (DEPTH, D_MODEL)),
        "w_ffn2_gate": nrm(24, (DEPTH, D_MODEL, D_FF), D_MODEL ** -0.5),
        "w_ffn2_up": nrm(25, (DEPTH, D_MODEL, D_FF), D_MODEL ** -0.5),
        "w_ffn2_down": nrm(26, (DEPTH, D_FF, D_MODEL), D_FF ** -0.5),
        "norm_final": gain(27, (D_MODEL,)),
    }


def reference(x_prompt, x_sample, state_conv, state_ssm_re, state_ssm_im,
              norm_ffn1, w_ffn1_gate, w_ffn1_up, w_ffn1_down, norm_mix, w_in, w_conv, w_conv_out,
              ssm_lambda_re, ssm_lambda_im, ssm_log_step, ssm_b_re, ssm_b_im, ssm_c_re, ssm_c_im, ssm_d,
              w_glu, w_o, norm_ffn2, w_ffn2_gate, w_ffn2_up, w_ffn2_down, norm_final):
    params = (norm_ffn1, w_ffn1_gate, w_ffn1_up, w_ffn1_down, norm_mix, w_in, w_conv, w_conv_out,
              ssm_lambda_re, ssm_lambda_im, ssm_log_step, ssm_b_re, ssm_b_im, ssm_c_re, ssm_c_im, ssm_d,
              w_glu, w_o, norm_ffn2, w_ffn2_gate, w_ffn2_up, w_ffn2_down, norm_final)
    y_prompt, conv_p, ssm_re_p, ssm_im_p = trunk(x_prompt, None, None, None, params)
    y_sample, conv_s, ssm_re_s, ssm_im_s = trunk(x_sample, state_conv, state_ssm_re, state_ssm_im, params)
    return (y_prompt, y_sample, conv_p, ssm_re_p, ssm_im_p, conv_s, ssm_re_s, ssm_im_s)
```

```python
import numpy as np
import concourse.bass as bass
import concourse.mybir as mybir
from concourse.bass_utils import run_bass_kernel_spmd

F32 = mybir.dt.float32
BF16 = mybir.dt.bfloat16
U8 = mybir.dt.uint8
AF = mybir.ActivationFunctionType
ALU = mybir.AluOpType

D = 1024
DFF = 2816
NF = 22
T = 16
NBO, NTO, NCO = 17, 2176, 136
NBP, NTP, NCP = 16, 2048, 128
OWN_GROUPS = [(0, 5), (5, 9), (9, 13), (13, 17)]
PRE_GROUPS = [(0, 4), (4, 8), (8, 12), (12, 16)]
PI = float(np.pi)


class Sched:
    ENG = ("pe", "act", "dve", "pool", "sp")

    def __init__(self, nc, sems, dsems):
        self.nc = nc
        self.sem = sems
        self.dsem = dsems
        self.ops = {e: [] for e in self.ENG}
        self.cnt = {e: 0 for e in self.ENG}
        self.dcnt = {q: [0] * len(dsems[q]) for q in dsems}
        self.drr = {q: 0 for q in dsems}
        self.seen = {e: {} for e in self.ENG}
        self.lw = {}
        self.rd = {}
        self.semobj = {}
        for e, s in sems.items():
            self.semobj[("c", e)] = s
        for q, l in dsems.items():
            for i, s in enumerate(l):
                self.semobj[("d", q, i)] = s

    def _deps(self, eng, r, w):
        need = {}
        for k in list(r) + list(w):
            if k in self.lw:
                s, v = self.lw[k]
                need[s] = max(need.get(s, 0), v)
        for k in w:
            for s, v in self.rd.get(k, {}).items():
                need[s] = max(need.get(s, 0), v)
        waits = []
        for s, v in need.items():
            if eng == "pe" and s == ("c", "pe"):
                continue
            if self.seen[eng].get(s, 0) >= v:
                continue
            self.seen[eng][s] = v
            waits.append((s, v))
        return waits

    def _commit(self, r, w, comp):
        s, v = comp
        for k in w:
            self.lw[k] = comp
            self.rd[k] = {}
        for k in r:
            d = self.rd.setdefault(k, {})
            d[s] = max(d.get(s, 0), v)

    defer = None

    def op(self, eng, fn, r=(), w=()):
        if self.defer is not None:
            self.defer.append(lambda: self._op(eng, fn, r, w))
            return
        self._op(eng, fn, r, w)

    def _op(self, eng, fn, r=(), w=()):
        waits = self._deps(eng, r, w)
        self.cnt[eng] += 1
        comp = (("c", eng), self.cnt[eng])
        self.ops[eng].append((waits, fn, comp, 1))
        self._commit(r, w, comp)

    defer_dma = None

    def dma(self, q, out, in_, r=(), w=(), **kw):
        if self.defer_dma is not None:
            self.defer_dma.append(lambda: self._dma(q, out, in_, r, w, **kw))
            return
        if self.defer is not None:
            self.defer.append(lambda: self._dma(q, out, in_, r, w, **kw))
            return
        self._dma(q, out, in_, r, w, **kw)

    def _dma(self, q, out, in_, r=(), w=(), **kw):
        i = self.drr[q]
        self.drr[q] = (i + 1) % len(self.dsem[q])
        s = ("d", q, i)
        waits = self._deps(q, r, w)
        prev = self.dcnt[q][i]
        if prev > 0 and self.seen[q].get(s, 0) < prev:
            self.seen[q][s] = prev
            waits.append((s, prev))
        self.dcnt[q][i] += 16
        comp = (s, self.dcnt[q][i])
        self.ops[q].append((waits, (lambda e, o=out, i_=in_, kw=kw: e.dma_start(out=o, in_=i_, **kw)), comp, 16))
        self._commit(r, w, comp)

    def barrier(self):
        allc = [(("c", e), self.cnt[e]) for e in ("pe", "act", "dve", "pool") if self.cnt[e] > 0]
        for q in self.dsem:
            for i, v in enumerate(self.dcnt[q]):
                if v > 0:
                    allc.append((("d", q, i), v))
        for e in self.ENG:
            waits = []
            for s, v in allc:
                if s == ("c", e):
                    continue
                if self.seen[e].get(s, 0) < v:
                    self.seen[e][s] = v
                    waits.append((s, v))
            if waits:
                self.ops[e].append((waits, None, None, 0))
        self.lw = {}
        self.rd = {}

    def emit(self, block):
        nc = self.nc

        def run(eng_name):
            def body(e):
                for waits, fn, comp, inc in self.ops[eng_name]:
                    for s, v in waits:
                        e.wait_ge(self.semobj[s], v)
                    if fn is not None:
                        ins = fn(e)
                        ins.then_inc(self.semobj[comp[0]], inc)
                if eng_name in self.dsem:
                    for i, v in enumerate(self.dcnt[eng_name]):
                        if v > 0:
                            e.wait_ge(self.semobj[("d", eng_name, i)], v)
            return body

        block.tensor(run("pe"))
        block.scalar(run("act"))
        block.vector(run("dve"))
        block.gpsimd(run("pool"))
        block.sync(run("sp"))


def tiles_of(nb):
    if nb <= 4:
        return [(0, nb * 128)]
    h = (nb + 1) // 2
    return [(0, h * 128), (h * 128, nb * 128)]


def build_program():
    nc = bass.Bass("TRN2", target_bir_lowering=False)

    def din(name, shape):
        return nc.dram_tensor(name, list(shape), F32, kind="ExternalInput").ap()

    def dout(name, shape):
        return nc.dram_tensor(name, list(shape), F32, kind="ExternalOutput").ap()

    xown = din("xown", [NTO, D]); xpre = din("xpre", [NTP, D])
    sconv = din("sconv", [8, D]); sre = din("sre", [128, 128]); sim = din("sim", [128, 128])
    g_ffn1 = din("norm_ffn1", [1, D]); g_mix = din("norm_mix", [1, D]); g_ffn2 = din("norm_ffn2", [1, D]); g_fin = din("norm_final", [1, D])
    w1g = din("w_ffn1_gate", [D, DFF]); w1u = din("w_ffn1_up", [D, DFF]); w1d = din("w_ffn1_down", [DFF, D])
    w2g = din("w_ffn2_gate", [D, DFF]); w2u = din("w_ffn2_up", [D, DFF]); w2d = din("w_ffn2_down", [DFF, D])
    w_in = din("w_in", [D, 6144]); w_conv = din("w_conv", [3, D]); w_co = din("w_conv_out", [D, D])
    lam_re = din("ssm_lambda_re", [64, 64]); lam_im = din("ssm_lambda_im", [64, 64]); lstep = din("ssm_log_step", [1, 64])
    b_re = din("ssm_b_re", [64, 64, 16]); b_im = din("ssm_b_im", [64, 64, 16])
    c_re = din("ssm_c_re", [64, 16, 64]); c_im = din("ssm_c_im", [64, 16, 64])
    ssm_d = din("ssm_d", [1, D]); w_glu = din("w_glu", [D, 2 * D]); w_o = din("w_o", [D, D])
    identd = din("ident", [128, 128]); maskd = din("qmask", [128, 128])
    y_o = dout("y", [NTO, D]); conv_o = dout("conv_o", [10, D]); hre_o = dout("hre_o", [5 * 32, 128]); him_o = dout("him_o", [5 * 32, 128])

    scr_wit = nc.dram_tensor("scr_wit", [8, 128, 4096], BF16).ap()
    scr_wbd = nc.dram_tensor("scr_wbd", [8, 128, 4096], BF16).ap()
    scr_g = nc.dram_tensor("scr_g", [22, 128, 1024], BF16).ap()
    scr_u = nc.dram_tensor("scr_u", [22, 128, 1024], BF16).ap()
    scr_d = nc.dram_tensor("scr_d", [8, 128, 2816], BF16).ap()
    ARENA = 210944
    import contextlib
    with contextlib.ExitStack() as es:
        arena = es.enter_context(nc.sbuf_tensor("arena", [128, ARENA], U8))
        PS = [es.enter_context(nc.psum_tensor("ps%d" % i, [128, 512], F32)) for i in range(8)]
        s_pe, s_act, s_dve, s_pool = [es.enter_context(nc.semaphore(n)) for n in ("s_pe", "s_act", "s_dve", "s_pool")]
        dps = [es.enter_context(nc.semaphore("dp%d" % i)) for i in range(8)]
        dss = [es.enter_context(nc.semaphore("ds%d" % i)) for i in range(8)]
        block = es.enter_context(nc.Block())
        S = Sched(nc, {"pe": s_pe, "act": s_act, "dve": s_dve, "pool": s_pool},
                  {"pool": dps, "sp": dss})

        off = [0]

        def carve(nbytes, dt, pattern=None, **kw):
            a = arena[:, off[0]:off[0] + nbytes].bitcast(dt)
            off[0] += (nbytes + 63) // 64 * 64
            assert off[0] <= ARENA, off[0]
            if pattern:
                a = a.rearrange(pattern, **kw)
            return a

        def carve_at(off_, nbytes, dt, pattern=None, **kw):
            assert off_ + nbytes <= ARENA, (off_, nbytes)
            a = arena[:, off_:off_ + nbytes].bitcast(dt)
            if pattern:
                a = a.rearrange(pattern, **kw)
            return a

        XRES = carve(NBO * D * 4, F32, "p (b d) -> p b d", d=D)
        GBC = carve(2 * D * 4, F32, "p (s d) -> p s d", d=D)
        UT = carve(8 * NTO * 2, BF16, "p (c t) -> p c t", t=NTO)
        UTP = UT.rearrange("p c (j k) -> p c j k", j=T)
        IDB = carve(128 * 2, BF16); IDF = carve(128 * 4, F32); MSK = carve(128 * 4, F32)
        WC = carve(3 * 8 * 4, F32, "p (k c) -> p k c", c=8); DV = carve(8 * 4, F32)
        SS = carve(32 * 4, F32); RSTD = carve(32 * 4, F32)
        BNr = carve(32 * 16 * 4, F32, "p (r c) -> p r c", c=16); BNi = carve(32 * 16 * 4, F32, "p (r c) -> p r c", c=16)
        CSr = carve(32 * 32 * 4, F32, "p (r c) -> p r c", c=32); CSi = carve(32 * 32 * 4, F32, "p (r c) -> p r c", c=32)
        CBr = carve(32 * 32 * 2, BF16, "p (r c) -> p r c", c=32); CBni = carve(32 * 32 * 2, BF16, "p (r c) -> p r c", c=32)
        PWr = carve(32 * 17 * 4, F32, "p (r j) -> p r j", j=17); PWi = carve(32 * 17 * 4, F32, "p (r j) -> p r j", j=17)
        SQr = carve(32 * 16 * 4, F32, "p (r j) -> p r j", j=16); SQi = carve(32 * 16 * 4, F32, "p (r j) -> p r j", j=16)
        tl = [carve(32 * 4, F32) for _ in range(14)]
        EPSB = carve(4 * 4, F32)
        COEF = carve(2 * 2 * 32 * 4, F32, "p (o i r) -> p o i r", o=2, i=2)
        SA = [carve(2 * 32 * 4, F32, "p (i r) -> p i r", i=2) for _ in range(2)]
        TPA = [carve(2 * 3 * 32 * 4, F32, "p (o i r) -> p o i r", o=2, i=3) for _ in range(2)]
        SS2 = carve(32 * 4, F32); RSTD2 = carve(32 * 4, F32)
        HF = carve(5 * 2 * 32 * 4, F32, "p (s i r) -> p s i r", s=5, i=2)
        ZPREV = carve(8 * 2 * 4, F32, "p (c k) -> p c k", k=2)
        SCT = carve(8 * 8 * 4, F32, "p (c k) -> p c k", k=8)
        CST = carve(8 * 10 * 4, F32, "p (c k) -> p c k", k=10)
        WORK0 = off[0]
        WORKSZ = ARENA - WORK0

        def W(off_, nbytes, dt, pattern=None, **kw):
            assert off_ + nbytes <= WORKSZ, (off_, nbytes, WORKSZ)
            return carve_at(WORK0 + off_, nbytes, dt, pattern, **kw)

        XN = [W(0, 8 * 640 * 2, BF16, "p (c t) -> p c t", t=640), W(10240, 8 * 512 * 2, BF16, "p (c t) -> p c t", t=512)]
        XNT = XN[0]
        SIL = [W(18432, 2048, F32)]
        XSBS = [W(20480, 2048, BF16), W(22528, 2048, BF16)]
        JNK = W(24576, 2048, BF16)
        HID = W(26624, 11 * 640 * 2, BF16, "p (f t) -> p f t", t=640)
        WGU = [W(40704 + i * 4096, 4096, BF16, "p (k g m) -> p k g m", g=2, m=128) for i in range(3)]
        WDS = [W(52992 + i * 5632, 5632, BF16, "p (f m) -> p f m", m=256) for i in range(2)]
        WBD = W(0, 8192, BF16, "p (j r m) -> p j r m", r=2, m=128)
        WIT = W(8192, 8192, BF16, "p (n m) -> p n m", m=128)
        TA = W(16384, 4096, F32); TB = W(20480, 4096, F32)
        XH = W(24576, 32 * 2 * NCO * 4, F32, "p (i r k) -> p i r k", i=2, k=NCO)
        XHP = XRES[:, 9:17, :].rearrange("p b d -> p (b d)").rearrange("p (i r k) -> p i r k", i=2, k=NCP)
        HB = W(0, 32 * 2 * NCO * 2, BF16, "p (i r k) -> p i r k", i=2, k=NCO)
        S4 = [W(17408 + i * 1024, 1024, F32, "p (i r s) -> p i r s", i=2, s=4) for i in range(2)]
        TP4A = [W(19456, 3072, F32, "p (o i r s) -> p o i r s", o=2, i=3, s=4)]
        WBD2 = W(17408, 8192, BF16, "p (j r m) -> p j r m", r=2, m=128)
        TA2 = W(25600, 4096, F32); TB2 = W(29696, 4096, F32)
        HT = W(0, 8 * 640 * 2, BF16, "p (c t) -> p c t", t=640)


        ident_bf = IDB
        psrr = [0]

        def psget():
            i = psrr[0]
            psrr[0] = (i + 1) % 8
            return i

        def mm_group(lst, r, w):
            def fn(e, lst=lst):
                ins = None
                for (o, l, rh, st, sp_, tp) in lst:
                    if tp is None:
                        ins = e.matmul(o, lhsT=l, rhs=rh, start=st, stop=sp_)
                    else:
                        ins = e.matmul(o, lhsT=l, rhs=rh, start=st, stop=sp_, tile_position=tp)
                return ins
            S.op("pe", fn, r=r, w=w)

        def tr_group(lst, r, w):
            def fn(e, lst=lst):
                ins = None
                for (o, i_, idt) in lst:
                    ins = e.transpose(o, i_, idt)
                return ins
            S.op("pe", fn, r=r, w=w)

        def act(out, in_, func, r, w, **kw):
            S.op("act", lambda e: e.activation(out=out, in_=in_, func=func, **kw), r=r, w=w)

        def tt(out, in0, in1, op, r, w, eng="dve"):
            S.op(eng, lambda e: e.tensor_tensor(out=out, in0=in0, in1=in1, op=op), r=r, w=w)

        def ts(out, in0, s1, s2, op0, op1, r, w, eng="dve"):
            if s2 is None:
                S.op(eng, lambda e: e.tensor_scalar(out=out, in0=in0, scalar1=s1, scalar2=None, op0=op0), r=r, w=w)
            else:
                S.op(eng, lambda e: e.tensor_scalar(out=out, in0=in0, scalar1=s1, scalar2=s2, op0=op0, op1=op1), r=r, w=w)

        def stt(out, in0, scalar, in1, op0, op1, r, w, eng="dve"):
            S.op(eng, lambda e: e.scalar_tensor_tensor(out=out, in0=in0, scalar=scalar, in1=in1, op0=op0, op1=op1), r=r, w=w)

        def cp(out, in_, r, w, eng="dve"):
            S.op(eng, lambda e: e.tensor_copy(out=out, in_=in_), r=r, w=w)

        def ms(ap, val, w, eng="dve"):
            S.op(eng, lambda e: e.memset(ap, val), r=(), w=w)

        def colblock(Wd_, c0, n=128):
            return Wd_[:, c0:c0 + n].rearrange("(k p) m -> p k m", p=128)

        bg = []

        def pump(n=1):
            for _ in range(n):
                if bg:
                    bg.pop(0)()

        def flush():
            while bg:
                bg.pop(0)()

        def atomic(f):
            if S.defer is None:
                f()
                return
            saved = S.defer
            tmp = []
            S.defer = tmp
            try:
                f()
            finally:
                S.defer = saved
            saved.append(lambda tmp=tmp: [t() for t in tmp])

        S.dma("pool", IDB, identd, w=["idb"])
        ms(EPSB, 1e-6, ["epsb"])
        bg_dma = []
        S.defer = bg
        S.defer_dma = bg_dma
        S.dma("sp", IDF, identd, w=["idf"])
        S.dma("sp", MSK, maskd, w=["msk"])
        with nc.allow_non_contiguous_dma(reason="tiny param loads"):
            pass
        S.dma("sp", WC, w_conv.rearrange("k (c p) -> p k c", p=128), w=["wc"], allow_slow_non_contiguous=True)
        S.dma("sp", DV, ssm_d.rearrange("o (c p) -> p (o c)", p=128), w=["dv"], allow_slow_non_contiguous=True)
        LRE, LIM, LS = tl[0], tl[1], tl[2]
        LAMT = XRES[0:32, 8, 0:384].rearrange("p (i m) -> p i m", m=128)
        LSS = XRES[0:32, 8, 384:386]
        S.dma("sp", LAMT[:, 0, :], lam_re.rearrange("(r e) p -> r (e p)", e=2), w=["lamt0"])
        S.dma("sp", LAMT[:, 1, :], lam_im.rearrange("(r e) p -> r (e p)", e=2), w=["lamt1"])
        S.dma("sp", LSS, lstep.rearrange("o (r e) -> r (o e)", e=2), w=["lss"], allow_slow_non_contiguous=True)
        cp(LAMT[:, 2, :].rearrange("p (e q) -> p e q", e=2), LSS.unsqueeze(2).to_broadcast([32, 2, 64]), r=["lss"], w=["lamt2"])

        def _lamtr():
            pi = psget()
            tr_group([(PS[pi][:, i_ * 32:(i_ + 1) * 32], LAMT[0:32, i_, :], IDF[0:32, 0:32]) for i_ in range(3)],
                     r=["lamt0", "lamt1", "lamt2", "idf"], w=["ps%d" % pi])
            cp(LRE, PS[pi][:, 0:32], r=["ps%d" % pi], w=["lre"])
            cp(LIM, PS[pi][:, 32:64], r=["ps%d" % pi], w=["lim"])
            cp(LS, PS[pi][:, 64:96], r=["ps%d" % pi], w=["ls"])
        atomic(_lamtr)
        for (BN_, bd_, blk0, kk_) in ((BNr, b_re, 9, "bnr"), (BNi, b_im, 13, "bni")):
            BT = XRES[32:64, blk0:blk0 + 2, :].rearrange("p b d -> p (b d)")
            S.dma("sp", BT, bd_.rearrange("(r e) p c -> r (e p c)", e=2), w=[kk_ + "t"])

            def _bntr(BN_=BN_, BT=BT, kk_=kk_):
                pi = psget()
                btv = BT.rearrange("p (m c) -> p c m", c=16)
                tr_group([(PS[pi][:, c_ * 32:(c_ + 1) * 32], btv[:, c_, :], IDF[32:64, 32:64]) for c_ in range(16)],
                         r=[kk_ + "t", "idf"], w=["ps%d" % pi])
                cp(BN_, PS[pi][:].rearrange("p (c r) -> p r c", c=16), r=["ps%d" % pi], w=[kk_])
            atomic(_bntr)

        CTr = XRES[:, 9:13, :].rearrange("p b d -> p (b d)").rearrange("p (r m) -> p r m", m=128)
        CTi = XRES[:, 13:17, :].rearrange("p b d -> p (b d)").rearrange("p (r m) -> p r m", m=128)
        S.defer = None
        ms(CTr[0:32], 0.0, ["ctr"]); ms(CTi[0:32], 0.0, ["cti"])
        S.defer = bg
        for (CT, cd, k) in ((CTr, c_re, "ctr"), (CTi, c_im, "cti")):
            cv = cd.rearrange("(r e) c p -> e c r p", e=2)
            for e_ in range(2):
                S.dma("sp", CT[16 * e_:16 * e_ + 16, :, 64 * e_:64 * e_ + 64], cv[e_], r=[], w=[k])
        for (CT, CS, k, ks) in ((CTr, CSr, "ctr", "csr"), (CTi, CSi, "cti", "csi")):
            for hb in range(2):
                def _ct(CT=CT, CS=CS, k=k, ks=ks, hb=hb):
                    pi = psget()
                    tr_group([(PS[pi][:, r_ * 32:(r_ + 1) * 32], CT[0:32, hb * 16 + r_, :], IDF[0:32, 0:32]) for r_ in range(16)],
                             r=[k, "idf"], w=["ps%d" % pi])
                    cp(CS[:, hb * 16:(hb + 1) * 16, :], PS[pi][:].rearrange("p (r c) -> p r c", c=32), r=["ps%d" % pi], w=[ks])
                atomic(_ct)
        cp(CBr, CSr, r=["csr"], w=["cbr"])
        ts(CBni, CSi, -1.0, None, ALU.mult, None, r=["csi"], w=["cbni"])

        DT, MAG, PH, PHC, SN, CSN, LR, LI, QR, QI, TM, TM2 = tl[3], tl[4], tl[5], tl[6], tl[7], tl[8], tl[9], tl[10], tl[11], tl[12], tl[13], tl[2]
        act(DT, LS, AF.Exp, r=["ls"], w=["dt"])
        tt(MAG, LRE, DT, ALU.mult, r=["lre", "dt"], w=["mag"])
        act(MAG, MAG, AF.Exp, r=["mag"], w=["mag"])
        tt(PH, LIM, DT, ALU.mult, r=["lim", "dt"], w=["ph"])
        ts(PHC, PH, PI / 2, None, ALU.add, None, r=["ph"], w=["phc"])
        for (P_, k) in ((PH, "ph"), (PHC, "phc")):
            for it in range(8):
                ts(TM, P_, PI, -2 * PI, ALU.is_gt, ALU.mult, r=[k], w=["tm"])
                tt(P_, P_, TM, ALU.add, r=[k, "tm"], w=[k])
        act(SN, PH, AF.Sin, r=["ph"], w=["sn"])
        act(CSN, PHC, AF.Sin, r=["phc"], w=["csn"])
        tt(LR, MAG, CSN, ALU.mult, r=["mag", "csn"], w=["lr"])
        tt(LI, MAG, SN, ALU.mult, r=["mag", "sn"], w=["li"])
        NR = DT
        ts(NR, LR, -1.0, None, ALU.add, None, r=["lr", "mag"], w=["dt"])
        DEN = MAG
        tt(DEN, LRE, LRE, ALU.mult, r=["lre", "lr", "li"], w=["mag"])
        tt(TM, LIM, LIM, ALU.mult, r=["lim"], w=["tm"])
        tt(DEN, DEN, TM, ALU.add, r=["mag", "tm"], w=["mag"])
        S.op("dve", lambda e: e.reciprocal(out=DEN, in_=DEN), r=["mag"], w=["mag"])
        tt(QR, NR, LRE, ALU.mult, r=["dt", "lre"], w=["qr"])
        tt(TM, LI, LIM, ALU.mult, r=["li", "lim"], w=["tm"])
        tt(QR, QR, TM, ALU.add, r=["qr", "tm"], w=["qr"])
        tt(QR, QR, DEN, ALU.mult, r=["qr", "mag"], w=["qr"])
        tt(QI, LI, LRE, ALU.mult, r=["li", "lre"], w=["qi"])
        tt(TM, NR, LIM, ALU.mult, r=["dt", "lim"], w=["tm"])
        tt(QI, QI, TM, ALU.subtract, r=["qi", "tm"], w=["qi"])
        tt(QI, QI, DEN, ALU.mult, r=["qi", "mag"], w=["qi"])
        ms(PWr[:, :, 0], 1.0, ["pw"]); ms(PWi[:, :, 0], 0.0, ["pw"])
        for j in range(16):
            tt(TM, PWr[:, :, j], LR, ALU.mult, r=["pw", "lr"], w=["tm"])
            tt(SN, PWi[:, :, j], LI, ALU.mult, r=["pw", "li"], w=["sn"])
            tt(PWr[:, :, j + 1], TM, SN, ALU.subtract, r=["tm", "sn"], w=["pw"])
            tt(TM, PWr[:, :, j], LI, ALU.mult, r=["pw", "li"], w=["tm"])
            tt(SN, PWi[:, :, j], LR, ALU.mult, r=["pw", "lr"], w=["sn"])
            tt(PWi[:, :, j + 1], TM, SN, ALU.add, r=["tm", "sn"], w=["pw"])
        AR, AI = CSN, PHC
        cp(AR, PWr[:, :, 16], r=["pw"], w=["ar"]); cp(AI, PWi[:, :, 16], r=["pw"], w=["ai"])
        for j in range(16):
            jj = T - 1 - j
            tt(TM, PWr[:, :, jj], QR, ALU.mult, r=["pw", "qr"], w=["tm"])
            tt(SN, PWi[:, :, jj], QI, ALU.mult, r=["pw", "qi"], w=["sn"])
            tt(SQr[:, :, j], TM, SN, ALU.subtract, r=["tm", "sn"], w=["sq"])
            tt(TM, PWr[:, :, jj], QI, ALU.mult, r=["pw", "qi"], w=["tm"])
            tt(SN, PWi[:, :, jj], QR, ALU.mult, r=["pw", "qr"], w=["sn"])
            tt(SQi[:, :, j], TM, SN, ALU.add, r=["tm", "sn"], w=["sq"])
        cp(COEF[:, 0, 0, :], AR, r=["ar"], w=["coef"]); cp(COEF[:, 1, 1, :], AR, r=["ar"], w=["coef"])
        cp(COEF[:, 1, 0, :], AI, r=["ai"], w=["coef"]); ts(COEF[:, 0, 1, :], AI, -1.0, None, ALU.mult, None, r=["ai"], w=["coef"])
        S.defer = None
        S.defer_dma = None

        def load_gain(slot, gd):
            S.dma("sp", GBC[:, slot, :], gd[0, :].partition_broadcast(128), w=["gbc%d" % slot])

        def norm_stats(xblk, nb, xkeys, SSx=SS, RSTDx=RSTD, sk="ss", rk="rstd"):
            ms(SSx, 0.0, [sk])
            for b in range(nb):
                act(JNK, xblk(b), AF.Square, r=[xkeys(b)], w=["jnk", sk], accum_out=SSx[:, b:b + 1])
            act(RSTDx[:, 0:nb], SSx[:, 0:nb], AF.Sqrt, r=[sk, "epsb"], w=[rk], scale=1.0 / D, bias=EPSB[:, 0:1])
            S.op("dve", lambda e: e.reciprocal(out=RSTDx[:, 0:nb], in_=RSTDx[:, 0:nb]), r=[rk], w=[rk])

        def norm_apply(xblk, nb, slot, dst, dkey, xkeys, RSTDx=RSTD, rk="rstd"):
            for b in range(nb):
                XSB = XSBS[b % 2]
                xk_ = "xsb%d" % (b % 2)
                stt(XSB, xblk(b), RSTDx[:, b:b + 1], GBC[:, slot, :], ALU.mult, ALU.mult, r=[xkeys(b), rk, "gbc%d" % slot], w=[xk_])
                pi = psget()
                pb = PS[pi][:].bitcast(BF16)
                tr_group([(pb[:, c * 128:(c + 1) * 128], XSB[:, c * 128:(c + 1) * 128], IDB) for c in range(8)],
                         r=[xk_, "idb"], w=["ps%d" % pi])
                act(dst[:, :, b * 128:(b + 1) * 128], pb.rearrange("p (c t) -> p c t", t=128), AF.Copy, r=["ps%d" % pi], w=[dkey])

        def norm_T(xblk, nb, slot, dst, dkey, xkeys):
            norm_stats(xblk, nb, xkeys)
            norm_apply(xblk, nb, slot, dst, dkey, xkeys)

        def wd_dma(Wd, half, qd, wc=None):
            ws = (half * 4 + qd) % 2
            i8 = half * 4 + qd
            if wc is not None and not wc["first"]:
                S.dma("sp", WDS[ws].rearrange("p f m -> p (f m)"), scr_d[i8], r=["scrd%d" % i8], w=["wds%d" % ws])
                return
            S.dma("pool", WDS[ws], Wd[half * 1408:(half + 1) * 1408, qd * 256:(qd + 1) * 256].rearrange("(f p) m -> p f m", p=128), w=["wds%d" % ws])
            if wc is not None:
                S.dma("sp", scr_d[i8], WDS[ws].rearrange("p f m -> p (f m)"), r=["wds%d" % ws], w=["scrd%d" % i8])

        def ffn_gu(half, XNg, xnk, nb, Wg, Wu, Wd, npump=0, hooks=None, wc=None):
            tls = tiles_of(nb)
            for fi in range(11):
                f = half * 11 + fi
                if fi == 3:
                    wd_dma(Wd, half, 0, wc); wd_dma(Wd, half, 1, wc)
                sl = f % 3
                if wc is not None and not wc["first"]:
                    S.dma("sp", WGU[sl][:, :, 0, :], scr_g[f].rearrange("p (k m) -> p k m", m=128), r=["scrg%d" % f], w=["wgu%dg" % sl])
                    S.dma("sp", WGU[sl][:, :, 1, :], scr_u[f].rearrange("p (k m) -> p k m", m=128), r=["scru%d" % f], w=["wgu%du" % sl])
                else:
                    S.dma("pool", WGU[sl][:, :, 0, :], colblock(Wg, f * 128), w=["wgu%dg" % sl])
                    S.dma("pool", WGU[sl][:, :, 1, :], colblock(Wu, f * 128), w=["wgu%du" % sl])
                    if wc is not None:
                        S.dma("sp", scr_g[f].rearrange("p (k m) -> p k m", m=128), WGU[sl][:, :, 0, :], r=["wgu%dg" % sl], w=["scrg%d" % f])
                        S.dma("sp", scr_u[f].rearrange("p (k m) -> p k m", m=128), WGU[sl][:, :, 1, :], r=["wgu%du" % sl], w=["scru%d" % f])
                for ti, (c0, c1) in enumerate(tls):
                    pa, pb_ = psget(), psget()
                    n = c1 - c0
                    mm_group([(PS[pa][:, 0:n], WGU[sl][:, k, 0, :], XNg[:, k, c0:c1], k == 0, k == 7, None) for k in range(8)],
                             r=["wgu%dg" % sl, xnk], w=["ps%d" % pa])
                    mm_group([(PS[pb_][:, 0:n], WGU[sl][:, k, 1, :], XNg[:, k, c0:c1], k == 0, k == 7, None) for k in range(8)],
                             r=["wgu%du" % sl, xnk], w=["ps%d" % pb_])
                    act(SIL[0][:, 0:n], PS[pa][:, 0:n], AF.Silu, r=["ps%d" % pa], w=["sil0"])
                    tt(HID[:, fi, c0:c1], SIL[0][:, 0:n], PS[pb_][:, 0:n], ALU.mult, r=["sil0", "ps%d" % pb_], w=["hid"])
                pump(npump)
                if hooks and fi in hooks:
                    hooks[fi]()

        def ffn_dn(half, xblk, nb, xkeys, Wd, wc=None):
            for qd in range(4):
                ws = (half * 4 + qd) % 2
                if qd >= 2:
                    wd_dma(Wd, half, qd, wc)
                for b in range(nb):
                    pi = psget()
                    mm_group([(PS[pi][:, 0:256], HID[:, fi, b * 128:(b + 1) * 128], WDS[ws][:, fi, :], fi == 0, fi == 10, None) for fi in range(11)],
                             r=["hid", "wds%d" % ws], w=["ps%d" % pi])
                    xs = xblk(b)[:, qd * 256:(qd + 1) * 256]
                    stt(xs, PS[pi][:, 0:256], 0.5, xs, ALU.mult, ALU.add, r=["ps%d" % pi, xkeys(b)], w=[xkeys(b)])

        def ffn(xblk, nb, xkeys, gd, Wg, Wu, Wd, npump=0):
            load_gain(0, gd)
            norm_T(xblk, nb, 0, XNT, "xnt", xkeys)
            for half in range(2):
                ffn_gu(half, XNT, "xnt", nb, Wg, Wu, Wd, npump)
                ffn_dn(half, xblk, nb, xkeys, Wd)

        def uproj(nb, tok0, src, skey):
            tls = tiles_of(nb)
            for c in range(8):
                sl = c % 3
                S.dma("pool", WGU[sl][:, :, 0, :], colblock(w_in, 3072 + c * 128), w=["wgu%dg" % sl])
                for (c0, c1) in tls:
                    pi = psget()
                    n = c1 - c0
                    mm_group([(PS[pi][:, 0:n], WGU[sl][:, k, 0, :], src[:, k, c0:c1], k == 0, k == 7, None) for k in range(8)],
                             r=["wgu%dg" % sl, skey], w=["ps%d" % pi])
                    k0_, k1_ = (tok0 + c0) // T, (tok0 + c1) // T
                    act(UTP[:, c, :, k0_:k1_], PS[pi][:, 0:n].rearrange("p (k j) -> p j k", j=T), AF.Copy, r=["ps%d" % pi], w=["ut"])

        def phase1(groups, xb_of, xk_of, npump_of, hoist_ok, tail=None, wc_of=None):
            n = len(groups)
            nb0 = groups[0][1] - groups[0][0]
            norm_stats(xb_of(0), nb0, xk_of(0))
            norm_apply(xb_of(0), nb0, 0, XN[0], "xn0", xk_of(0))

            def NMs(g):
                nb = groups[g][1] - groups[g][0]
                norm_stats(xb_of(g), nb, xk_of(g), SS2, RSTD2, "ss2", "rstd2")

            def NMa(g):
                nb = groups[g][1] - groups[g][0]
                norm_apply(xb_of(g), nb, 1, XN[g % 2], "xn%d" % (g % 2), xk_of(g), RSTD2, "rstd2")

            def UP(g):
                nb = groups[g][1] - groups[g][0]
                uproj(nb, groups[g][0] * 128, XN[g % 2], "xn%d" % (g % 2))

            for g in range(n):
                nb = groups[g][1] - groups[g][0]
                XNg, xnk = XN[g % 2], "xn%d" % (g % 2)
                h0 = {2: (lambda g=g: NMa(g - 1))} if g > 0 else None
                wc = wc_of(g) if wc_of else None
                ffn_gu(0, XNg, xnk, nb, w1g, w1u, w1d, npump_of(g, 0), h0, wc)
                if g > 0:
                    UP(g - 1)
                ffn_dn(0, xb_of(g), nb, xk_of(g), w1d, wc)
                h1 = None
                hoisted = (g + 1 < n) and hoist_ok(g + 1)
                if hoisted:
                    nbn = groups[g + 1][1] - groups[g + 1][0]
                    h1 = {1: (lambda g=g, nbn=nbn: norm_stats(xb_of(g + 1), nbn, xk_of(g + 1))),
                          6: (lambda g=g, nbn=nbn: norm_apply(xb_of(g + 1), nbn, 0, XN[(g + 1) % 2], "xn%d" % ((g + 1) % 2), xk_of(g + 1)))}
                ffn_gu(1, XNg, xnk, nb, w1g, w1u, w1d, npump_of(g, 1), h1, wc)
                ffn_dn(1, xb_of(g), nb, xk_of(g), w1d, wc)
                NMs(g)
                if (g + 1 < n) and not hoisted:
                    if tail:
                        tail(g)
                    nbn = groups[g + 1][1] - groups[g + 1][0]
                    norm_stats(xb_of(g + 1), nbn, xk_of(g + 1))
                    norm_apply(xb_of(g + 1), nbn, 0, XN[(g + 1) % 2], "xn%d" % ((g + 1) % 2), xk_of(g + 1))
            NMa(n - 1)
            UP(n - 1)

        wslot = [0]
        WCOL_OFF = [None]

        def gen_wbd_part(ch, WBDx, ri, eng, TAx, TBx, nsplit, ka, kb, wpre="wbd"):
            nq = 4 // nsplit
            for qs in range(nsplit):
                p0_ = ch * 4 + qs * nq
                sr = SQr[:, p0_:p0_ + nq, :].unsqueeze(3).to_broadcast([128, nq, 16, 16])
                si = SQi[:, p0_:p0_ + nq, :].unsqueeze(3).to_broadcast([128, nq, 16, 16])
                br = BNr[:, p0_:p0_ + nq, :].unsqueeze(2).to_broadcast([128, nq, 16, 16])
                bi = BNi[:, p0_:p0_ + nq, :].unsqueeze(2).to_broadcast([128, nq, 16, 16])
                ta = TAx.rearrange("p (q j c) -> p q j c", q=nq, j=16)
                tb = TBx.rearrange("p (q j c) -> p q j c", q=nq, j=16)
                if ri == 0:
                    tt(ta, sr, br, ALU.mult, r=["sq", "bnr"], w=ka, eng=eng)
                    tt(tb, si, bi, ALU.mult, r=["sq", "bni"], w=kb, eng=eng)
                    op = ALU.subtract
                else:
                    tt(ta, sr, bi, ALU.mult, r=["sq", "bni"], w=ka, eng=eng)
                    tt(tb, si, br, ALU.mult, r=["sq", "bnr"], w=kb, eng=eng)
                    op = ALU.add
                for e_ in range(2):
                    o = WBDx[64 * e_:64 * e_ + 64, :, ri, :].rearrange("p j (q e c) -> p q j e c", q=4, e=2)[:, qs * nq:(qs + 1) * nq, :, e_, :]
                    tt(o, ta[64 * e_:64 * e_ + 64], tb[64 * e_:64 * e_ + 64], op, r=ka + kb, w=["%s%d" % (wpre, ri)], eng=eng)

        TAP = W(59392, 2048, F32); TBP = W(61440, 2048, F32)

        def gen_wbd(ch):
            gen_wbd_part(ch, WBD, 0, "dve", TA, TB, 1, ["ta"], ["tb"])
            gen_wbd_part(ch, WBD, 1, "pool", TAP, TBP, 2, ["tap"], ["tbp"])

        GX = XRES[:, 9:17, :].rearrange("p b d -> p (b d)")
        GWBDS = [GX[:, 0:2048].bitcast(BF16).rearrange("p (j r m) -> p j r m", r=2, m=128),
                 GX[:, 6144:8192].bitcast(BF16).rearrange("p (j r m) -> p j r m", r=2, m=128)]
        GWIT = GX[:, 2048:4096].bitcast(BF16).rearrange("p (n m) -> p n m", m=128)
        GTA = GX[:, 4096:4608]; GTB = GX[:, 4608:5120]; GTAP = GX[:, 5120:5632]; GTBP = GX[:, 5632:6144]

        def gen_all():
            ms(GX, 0.0, ["ga0", "ga1", "gb0", "gb1", "ctr", "cti", "wit0", "wit1", "wit2", "wit3", "ta", "tb", "tap", "tbp"], eng="pool")

            def elem(ch):
                G = GWBDS[ch % 2]
                wp = "ga" if ch % 2 == 0 else "gb"
                gen_wbd_part(ch, G, 0, "dve", GTA, GTB, 2, ["ta"], ["tb"], wpre=wp)
                gen_wbd_part(ch, G, 1, "pool", GTAP, GTBP, 2, ["tap"], ["tbp"], wpre=wp)

            def pepart(ch):
                G = GWBDS[ch % 2]
                wp = "ga" if ch % 2 == 0 else "gb"
                S.dma("pool", scr_wbd[ch], G.rearrange("p j r m -> p (j r m)"), r=[wp + "0", wp + "1"], w=["scrb%d" % ch])
                wv = G.rearrange("p j r m -> p (j r) m")
                for n_ in range(4):
                    def _tw(n_=n_, wv=wv, wp=wp):
                        pi = psget()
                        pb = PS[pi][:].bitcast(BF16)
                        tr_group([(pb[:, s_ * 128:(s_ + 1) * 128], wv[:, n_ * 8 + s_, :], IDB) for s_ in range(8)],
                                 r=[wp + "0", wp + "1", "idb"], w=["ps%d" % pi])
                        if n_ % 2 == 0:
                            act(GWIT[:, n_ * 8:(n_ + 1) * 8, :], pb.rearrange("p (s m) -> p s m", m=128), AF.Copy, r=["ps%d" % pi], w=["wit%d" % n_])
                        else:
                            cp(GWIT[:, n_ * 8:(n_ + 1) * 8, :], pb.rearrange("p (s m) -> p s m", m=128), r=["ps%d" % pi], w=["wit%d" % n_])
                    atomic(_tw)
                S.dma("pool", scr_wit[ch], GWIT.rearrange("p n m -> p (n m)"), r=["wit0", "wit1", "wit2", "wit3"], w=["scrw%d" % ch])

            elem(0)
            for ch in range(8):
                if ch + 1 < 8:
                    elem(ch + 1)
                pepart(ch)

        WITS = [WIT, WBD.rearrange("p j r m -> p (j r) m")]

        def phase_i(nck, XD):
            for ch in range(8):
                Wt = WITS[ch % 2]
                wk = "witb%d" % (ch % 2)
                S.dma("sp", Wt.rearrange("p n m -> p (n m)"), scr_wit[ch], r=["scrw%d" % ch], w=[wk])
                pis = [psget() for _ in range(4)]
                lst = []
                for ri in range(2):
                    for j in range(T):
                        for q in range(4):
                            lst.append((PS[pis[q]][:, ri * 256:ri * 256 + nck], Wt[32 * q:32 * q + 32, j * 2 + ri, :],
                                        UTP[32 * q:32 * q + 32, ch, j, 0:nck], j == 0, j == T - 1, (32 * q, 0)))
                mm_group(lst, r=[wk, "ut"], w=["ps%d" % p_ for p_ in pis])
                for q in range(4):
                    act(XD[:, :, ch * 4 + q, 0:nck], PS[pis[q]][:].rearrange("p (i k) -> p i k", k=256)[:, :, 0:nck], AF.Copy,
                        r=["ps%d" % pis[q]], w=["xh"])

        def scan_prefill(k_idx, xk_, TPl=TPA, kp="sa"):
            T3 = TPl[k_idx % len(TPl)]
            tk = "tp%s%d" % (kp, k_idx % len(TPl))
            o = T3[:, :, 2, :] if len(T3.shape) == 4 else T3[:, :, 2, :, :]
            act(o, xk_, AF.Copy, r=["xh"], w=[tk + "x"])

        def scan_step(cur, k_idx, coef=COEF, SAl=SA, TPl=TPA, bshape=[128, 2, 2, 32], kp="sa"):
            nxt = 1 - cur
            T3 = TPl[k_idx % len(TPl)]
            tk = "tp%s%d" % (kp, k_idx % len(TPl))
            if len(bshape) == 4:
                tt(T3[:, :, 0:2, :], coef, SAl[cur].unsqueeze(1).to_broadcast(bshape), ALU.mult, r=["coef", "%s%d" % (kp, cur)], w=[tk])
                rin = T3.rearrange("p o i r -> p o r i")
            else:
                tt(T3[:, :, 0:2, :, :], coef, SAl[cur].unsqueeze(1).to_broadcast(bshape), ALU.mult, r=["coef", "%s%d" % (kp, cur)], w=[tk])
                rin = T3.rearrange("p o i r s -> p o r s i")
            S.op("dve", lambda e, o_=SAl[nxt], rin=rin: e.tensor_reduce(out=o_, in_=rin, axis=mybir.AxisListType.X, op=ALU.add),
                 r=[tk, tk + "x"], w=["%s%d" % (kp, nxt)])
            return nxt

        xpre_v = xpre.rearrange("(b p) d -> p b d", p=128)
        for gi, (b0, b1) in enumerate(PRE_GROUPS):
            o = 4 * (gi % 2)
            if gi < 2:
                S.dma("sp", XRES[:, o:o + 4, :], xpre_v[:, b0:b1, :], w=["x%d" % (o + b) for b in range(4)])

        def pre_xb(g):
            return lambda b, o=4 * (g % 2): XRES[:, o + b, :]

        def pre_xk(g):
            return lambda b, o=4 * (g % 2): "x%d" % (o + b)

        def pre_tail(g):
            pass

        _orig_uproj = uproj

        def uproj_pre(nb, tok0, src, skey):
            _orig_uproj(nb, tok0, src, skey)
            gdone = tok0 // 512
            if gdone + 2 < len(PRE_GROUPS):
                b0_, b1_ = PRE_GROUPS[gdone + 2]
                o = 4 * (gdone % 2)
                S.dma("sp", XRES[:, o:o + 4, :], xpre_v[:, b0_:b1_, :], w=["x%d" % (o + b) for b in range(4)])

        uproj = uproj_pre
        load_gain(0, g_ffn1)
        load_gain(1, g_mix)
        for th in bg_dma:
            th()
        S.defer = bg
        gen_all()
        S.defer = None
        phase1(PRE_GROUPS, pre_xb, pre_xk, lambda g, h: (0 if (g, h) == (0, 0) else (8 if (g, h) in ((0, 1), (1, 0), (1, 1), (2, 0)) else 6)), lambda g: True, wc_of=lambda g: {"first": g == 0})
        uproj = _orig_uproj
        flush()
        XL = XN[(len(PRE_GROUPS) - 1) % 2]
        xlk = "xn%d" % ((len(PRE_GROUPS) - 1) % 2)
        for c in range(8):
            S.dma("pool", WGU[0][:, :, 0, :], colblock(w_in, c * 128), w=["wgu0g"])
            S.dma("pool", WGU[0][:, :, 1, :], colblock(w_in, 1024 + c * 128), w=["wgu0u"])
            pa, pb_ = psget(), psget()
            lo = 4 * 128 - 32
            mm_group([(PS[pa][:, 0:32], WGU[0][:, k, 0, :], XL[:, k, lo:lo + 32], k == 0, k == 7, None) for k in range(8)],
                     r=["wgu0g", xlk], w=["ps%d" % pa])
            mm_group([(PS[pb_][:, 0:32], WGU[0][:, k, 1, :], XL[:, k, lo:lo + 32], k == 0, k == 7, None) for k in range(8)],
                     r=["wgu0u", xlk], w=["ps%d" % pb_])
            act(SIL[0][:, 0:2], PS[pa][:, 30:32], AF.Copy, r=["ps%d" % pa], w=["sil0"])
            tt(ZPREV[:, c, :], SIL[0][:, 0:2], PS[pb_][:, 30:32], ALU.mult, r=["sil0", "ps%d" % pb_], w=["zprev"])
        S.barrier()
        phase_i(NCP, XHP)
        S.barrier()
        ms(SA[0].rearrange("p i r -> p (i r)"), 0.0, ["sa0"])
        S.defer = bg
        cur = 0
        scan_prefill(0, XHP[:, :, :, 0])
        for k in range(NCP):
            if k + 1 < NCP:
                scan_prefill(k + 1, XHP[:, :, :, k + 1])
            cur = scan_step(cur, k)
        S.defer = None
        S.defer_dma = None
        pre_cur = cur

        xown_v = xown.rearrange("(b p) d -> p b d", p=128)
        for gi, (b0, b1) in enumerate(OWN_GROUPS[:2]):
            S.dma("sp", XRES[:, b0:b1, :], xown_v[:, b0:b1, :], w=["x%d" % b for b in range(b0, b1)])

        def own_xb(g):
            return lambda b, b0=OWN_GROUPS[g][0]: XRES[:, b0 + b, :]

        def own_xk(g):
            return lambda b, b0=OWN_GROUPS[g][0]: "x%d" % (b0 + b)

        def own_tail(g):
            if g == 1:
                flush()
                for (b0, b1) in OWN_GROUPS[2:]:
                    S.dma("sp", XRES[:, b0:b1, :], xown_v[:, b0:b1, :], w=["x%d" % b for b in range(b0, b1)] + ["xh"])

        load_gain(0, g_ffn1)
        load_gain(1, g_mix)
        phase1(OWN_GROUPS, own_xb, own_xk, lambda g, h: (9 if g < 2 else 0), lambda g: g != 2, tail=own_tail, wc_of=lambda g: {"first": False})
        flush()
        S.barrier()
        phase_i(NCO, XH)
        S.barrier()
        SIN = W(22528, 2 * 128 * 4, F32, "p (i m) -> p i m", m=128)
        S.dma("sp", SIN[:, 0, :], sre, w=["sin"]); S.dma("sp", SIN[:, 1, :], sim, w=["sin"])
        pi = psget()
        tr_group([(PS[pi][:, i_ * 128:(i_ + 1) * 128], SIN[:, i_, :], IDF) for i_ in range(2)], r=["sin", "idf"], w=["ps%d" % pi])
        cp(S4[0], PS[pi][:, 0:256].rearrange("p (i s r) -> p i r s", i=2, s=4), r=["ps%d" % pi], w=["sb0"])
        cur = pre_cur
        scan_prefill(0, XH[:, :, :, 0])
        for k in range(128):
            act(HB[:, :, :, k], SA[cur], AF.Copy, r=["sa%d" % cur], w=["hb"])
            if k + 1 < 128:
                scan_prefill(k + 1, XH[:, :, :, k + 1])
            cur = scan_step(cur, k)
        cp(HF[:, 0, :, :], SA[cur], r=["sa%d" % cur], w=["hf"])
        coef4 = COEF.unsqueeze(4).to_broadcast([128, 2, 2, 32, 4])
        c4 = 0
        for st in range(2):
            act(HB[:, :, :, 128 + st:128 + st + 7:2], S4[c4], AF.Copy, r=["sb%d" % c4], w=["hb"])
            scan_prefill(st, XH[:, :, :, 128 + st:128 + st + 7:2], TPl=TP4A, kp="sb")
            c4 = scan_step(c4, st, coef=coef4, SAl=S4, TPl=TP4A, bshape=[128, 2, 2, 32, 4], kp="sb")
        for s_ in range(4):
            cp(HF[:, 1 + s_, :, :], S4[c4][:, :, :, s_], r=["sb%d" % c4], w=["hf"])
        HOUT = W(24576, 10 * 128 * 4, F32, "p (n m) -> p n m", m=128)
        S.barrier()
        for ri in range(2):
            n0 = 5 * ri
            pi = psget()
            tr_group([(PS[pi][0:32, s_ * 128:(s_ + 1) * 128], HF[:, s_, ri, :], IDF) for s_ in range(4)], r=["hf", "idf"], w=["ps%d" % pi])
            cp(HOUT[0:32, n0:n0 + 4, :], PS[pi][0:32, :].rearrange("p (s m) -> p s m", m=128), r=["ps%d" % pi], w=["hout"])
            pi = psget()
            tr_group([(PS[pi][0:32, 0:128], HF[:, 4, ri, :], IDF)], r=["hf", "idf"], w=["ps%d" % pi])
            cp(HOUT[0:32, n0 + 4, :], PS[pi][0:32, 0:128], r=["ps%d" % pi], w=["hout"])
        S.dma("sp", hre_o.rearrange("(s r) m -> r s m", r=32), HOUT[0:32, 0:5, :], r=["hout"])
        S.dma("sp", him_o.rearrange("(s r) m -> r s m", r=32), HOUT[0:32, 5:10, :], r=["hout"])
        S.barrier()

        KDS = [W(33792 + i * 4096, 4096, BF16, "p (d m) -> p d m", m=128) for i in range(2)]
        CLH = [W(46080, 4096, BF16, "p (q j i c) -> p q j i c", q=4, j=8, i=2),
               W(41984, 4096, BF16, "p (q j i c) -> p q j i c", q=4, j=8, i=2)]
        CTS = [W(50176 + i * 2048, 2048, F32, "p (q j c) -> p q j c", q=2, j=8) for i in range(4)]
        TAD = W(50176, 4096, F32); TBD = W(54272, 4096, F32)
        GT0 = W(58368, 512, F32); DG = W(58880, 512, F32)
        TMPU = W(59392, T * NCO * 2, BF16, "p (j k) -> p j k", j=T)
        WBD = WBD2

        def ssm_A(ch):
            KD = KDS[ch % 2]
            kk = "kd%d" % (ch % 2)
            S.dma("sp", WBD2.rearrange("p j r m -> p (j r m)"), scr_wbd[ch], r=["scrb%d" % ch], w=["wbd0", "wbd1"])
            kps = []
            for n_ in range(4):
                pi = psget(); kps.append(pi)
                lst = []
                for s_ in range(4):
                    d = n_ * 4 + s_
                    j = T - 1 - d
                    lst.append((PS[pi][:, s_ * 128:(s_ + 1) * 128], WBD[:, j, 0, :], CBr[:, ch * 4:(ch + 1) * 4, :].rearrange("p r c -> p (r c)"), True, False, None))
                    lst.append((PS[pi][:, s_ * 128:(s_ + 1) * 128], WBD[:, j, 1, :], CBni[:, ch * 4:(ch + 1) * 4, :].rearrange("p r c -> p (r c)"), False, True, None))
                mm_group(lst, r=["wbd0", "wbd1", "cbr", "cbni"], w=["ps%d" % pi])
            ts(DG, IDF, DV[:, ch:ch + 1], None, ALU.mult, None, r=["idf", "dv"], w=["dg"])
            for n_ in range(4):
                pi = kps[n_]
                psv = PS[pi][:].rearrange("p (s m) -> p s m", m=128)
                s0_ = 0
                if n_ == 0:
                    tt(GT0[:, 0:128], PS[pi][:, 0:128], MSK, ALU.mult, r=["ps%d" % pi, "msk"], w=["gt0"])
                    tt(KD[:, 0, :], GT0[:, 0:128], DG, ALU.add, r=["gt0", "dg"], w=[kk])
                    s0_ = 1
                for qq in range(4):
                    act(KD[:, n_ * 4 + s0_:(n_ + 1) * 4, 32 * qq:32 * qq + 32], psv[:, s0_:4, 32 * qq:32 * qq + 32], AF.Copy,
                        r=["ps%d" % pi, "msk"], w=[kk], scale=MSK[:, 32 * qq:32 * qq + 1])

        CTP = [W(25600 + i * 2048, 2048, F32, "p (q j c) -> p q j c", q=2, j=8) for i in range(4)]

        def ssm_cl(ch, hi):
            CLx = CLH[hi]
            ck = "cl%d" % hi
            j0 = 1 + 8 * hi
            for qh in range(2):
                on_dve = (hi == 1) or (qh == 1)
                eng = "dve" if on_dve else "pool"
                CTx = CTS if on_dve else CTP
                kx = ["ct0", "ct1", "ct2", "ct3"] if on_dve else ["cp0", "cp1", "cp2", "cp3"]
                p0_ = ch * 4 + qh * 2
                l1r = PWr[:, p0_:p0_ + 2, j0:j0 + 8].unsqueeze(3).to_broadcast([128, 2, 8, 32])
                l1i = PWi[:, p0_:p0_ + 2, j0:j0 + 8].unsqueeze(3).to_broadcast([128, 2, 8, 32])
                cr = CSr[:, p0_:p0_ + 2, :].unsqueeze(2).to_broadcast([128, 2, 8, 32])
                ci = CSi[:, p0_:p0_ + 2, :].unsqueeze(2).to_broadcast([128, 2, 8, 32])
                clh = CLx[:, qh * 2:qh * 2 + 2]
                tt(CTx[0], cr, l1r, ALU.mult, r=["csr", "pw"], w=[kx[0]], eng=eng)
                tt(CTx[1], ci, l1i, ALU.mult, r=["csi", "pw"], w=[kx[1]], eng=eng)
                tt(CTx[2], cr, l1i, ALU.mult, r=["csr", "pw"], w=[kx[2]], eng=eng)
                tt(CTx[3], ci, l1r, ALU.mult, r=["csi", "pw"], w=[kx[3]], eng=eng)
                tt(clh[:, :, :, 0, :], CTx[0], CTx[1], ALU.subtract, r=[kx[0], kx[1]], w=[ck], eng=eng)
                if eng == "dve":
                    stt(clh[:, :, :, 1, :], CTx[2], -1.0, CTx[3], ALU.mult, ALU.subtract, r=[kx[2], kx[3]], w=[ck], eng=eng)
                else:
                    tt(CTx[2], CTx[2], CTx[3], ALU.add, r=[kx[2], kx[3]], w=[kx[2]], eng=eng)
                    ts(clh[:, :, :, 1, :], CTx[2], -1.0, None, ALU.mult, None, r=[kx[2]], w=[ck], eng=eng)

        def ssm_Y(ch, j):
            KD = KDS[ch % 2]
            hi = j // 8
            CLx = CLH[hi]
            pi = psget()
            lst = []
            for d in range(j + 1):
                lst.append((PS[pi][:, 0:NCO], KD[:, d, :], UTP[:, ch, j - d, :], d == 0, False, None))
            for ri in range(2):
                for q in range(4):
                    lst.append((PS[pi][32 * q:32 * q + 32, 0:NCO], CLx[:, q, j % 8, ri, :], HB[:, ri, ch * 4 + q, :], False, ri == 1, (0, 32 * q)))
            mm_group(lst, r=["kd%d" % (ch % 2), "cl%d" % hi, "hb"] + ["ut%d_%d" % (ch, jj_) for jj_ in range(j + 1)], w=["ps%d" % pi])
            act(UTP[:, ch, j, :], PS[pi][:, 0:NCO], AF.Gelu_apprx_tanh, r=["ps%d" % pi], w=["ut%d_%d" % (ch, j)])

        ssm_A(0); ssm_cl(0, 1); ssm_cl(0, 0)
        for ch in range(8):
            for j in range(15, 7, -1):
                ssm_Y(ch, j)
            if ch + 1 < 8:
                ssm_cl(ch + 1, 1)
            for j in range(7, 4, -1):
                ssm_Y(ch, j)
            if ch + 1 < 8:
                ssm_A(ch + 1)
            for j in range(4, -1, -1):
                ssm_Y(ch, j)
            if ch + 1 < 8:
                ssm_cl(ch + 1, 0)
            uk = ["ut%d_%d" % (ch, jj_) for jj_ in range(T)]
            act(TMPU, UTP[:, ch, :, :], AF.Copy, r=uk, w=["tmpu"])
            act(UT[:, ch, :].rearrange("p (k j) -> p j k", j=T), TMPU, AF.Copy, r=["tmpu"], w=uk)
        S.barrier()

        HT0 = W(0, 8 * 640 * 2, BF16, "p (c t) -> p c t", t=640)
        ZC0 = W(10240, 8 * 640 * 2, BF16, "p (c t) -> p c t", t=640)
        HT1 = W(10240, 8 * 512 * 2, BF16, "p (c t) -> p c t", t=512)
        ZC1 = W(0, 8 * 512 * 2, BF16, "p (c t) -> p c t", t=512)
        MRG = W(20480, 8 * 640 * 2, BF16, "p (c t) -> p c t", t=640)
        NWS = 7
        WS = [W(30720 + i * 2048, 2048, BF16, "p (k m) -> p k m", m=128) for i in range(NWS)]
        WOBS = [W(45056 + i * 4096, 8 * 256 * 2, BF16, "p (k m) -> p k m", m=256) for i in range(2)]
        ZB = W(53248, 2816, F32); VS = W(56064, 2048, F32); OB = W(58112, 2816, F32)
        SGC = W(53248, 2048, F32); M1 = W(55296, 2048, F32); SB_ = W(57344, 2048, F32); SGS = W(59392, 2048, F32); TT_ = W(61440, 2048, F32)
        SCI = W(53248, 4096, F32)
        S.dma("sp", SCI[0:8, :], sconv, w=["sci"])
        pi = psget()
        tr_group([(PS[pi][:, c * 8:(c + 1) * 8], SCI[0:8, c * 128:(c + 1) * 128], IDF[0:8, 0:8]) for c in range(8)], r=["sci", "idf"], w=["ps%d" % pi])
        cp(SCT, PS[pi][:, 0:64].rearrange("p (c k) -> p c k", k=8), r=["ps%d" % pi], w=["sct"])
        ms(CST.rearrange("p c k -> p (c k)"), 0.0, ["cst"])
        S.barrier()
        y_v = y_o.rearrange("(b p) d -> p b d", p=128)
        load_gain(1, g_mix)

        def final_norm(gi_):
            b0_, b1_ = OWN_GROUPS[gi_]
            nb_ = b1_ - b0_
            xb_ = lambda b, b0_=b0_: XRES[:, b0_ + b, :]
            xk_ = lambda b, b0_=b0_: "x%d" % (b0_ + b)
            load_gain(0, g_fin)
            norm_stats(xb_, nb_, xk_)
            for b in range(nb_):
                stt(xb_(b), xb_(b), RSTD[:, b:b + 1], GBC[:, 0, :], ALU.mult, ALU.mult, r=[xk_(b), "rstd", "gbc0"], w=[xk_(b)])
                S.dma("sp", y_v[:, b0_ + b, :], xb_(b), r=[xk_(b)])

        for gi, (b0, b1) in enumerate(OWN_GROUPS):
            nb = b1 - b0
            ntok = nb * 128
            tok0 = b0 * 128
            xb = lambda b, b0=b0: XRES[:, b0 + b, :]
            xk = lambda b, b0=b0: "x%d" % (b0 + b)
            last = gi == len(OWN_GROUPS) - 1
            if gi == 0:
                HTg, ZCg, htk = HT0, ZC0, "xnt"
                norm_T(xb, nb, 1, HTg, htk, xk)
            else:
                HTg, ZCg, htk = HT1, ZC1, "xn1"
                S.defer = bg
                final_norm(gi - 1)
                S.defer = None
            npr = ntok - 128 if last else ntok
            segs = [(0, npr, "p")] + ([(npr + 32 * s_, 32, s_) for s_ in range(4)] if last else [])
            tls = tiles_of(nb)
            for c in range(8):
                sl3 = [(3 * c + i_) % NWS for i_ in range(3)]
                for i_, base in enumerate((0, 1024, 2048)):
                    S.dma("pool", WS[sl3[i_]], colblock(w_in, base + c * 128), w=["ws%d" % sl3[i_]])
                pump(3)
                for (c0, c1) in tls:
                    n = c1 - c0
                    pa, pb_, pc = psget(), psget(), psget()
                    for pp, sl in ((pa, sl3[0]), (pb_, sl3[1]), (pc, sl3[2])):
                        mm_group([(PS[pp][:, 0:n], WS[sl][:, k, :], HTg[:, k, c0:c1], k == 0, k == 7, None) for k in range(8)],
                                 r=["ws%d" % sl, htk], w=["ps%d" % pp])
                    act(VS[:, 0:n], PS[pa][:, 0:n], AF.Copy, r=["ps%d" % pa], w=["vs"])
                    sg = [(s0, ln, kd) for (s0, ln, kd) in segs if s0 < c1 and s0 + ln > c0]
                    sg = [(i_, s0, ln, kd) for i_, (s0, ln, kd) in enumerate(sg)]
                    lo_pad = None
                    for (i_, s0, ln, kd) in sg:
                        a0, a1 = max(s0, c0), min(s0 + ln, c1)
                        zoff = 2 * (i_ + 1) + a0 - c0 if True else 0
                        if a0 == s0:
                            if kd == "p":
                                cp(ZB[:, zoff - 2:zoff], ZPREV[:, c, :], r=["zprev"], w=["zb"])
                            else:
                                cp(ZB[:, zoff - 2:zoff], SCT[:, c, 2 * kd:2 * kd + 2], r=["sct"], w=["zb"])
                        else:
                            cp(ZB[:, zoff - 2:zoff], ZPREV[:, c, :], r=["zprev"], w=["zb"])
                        tt(ZB[:, zoff:zoff + a1 - a0], VS[:, a0 - c0:a1 - c0], PS[pb_][:, a0 - c0:a1 - c0], ALU.mult, r=["vs", "ps%d" % pb_], w=["zb"])
                        if kd == "p":
                            cp(ZPREV[:, c, :], ZB[:, zoff + a1 - a0 - 2:zoff + a1 - a0], r=["zb"], w=["zprev"])
                            if a1 == s0 + ln and last:
                                cp(CST[:, c, 0:2], ZB[:, zoff + a1 - a0 - 2:zoff + a1 - a0], r=["zb"], w=["cst"])
                        else:
                            cp(CST[:, c, 2 + 2 * kd:4 + 2 * kd], ZB[:, zoff + a1 - a0 - 2:zoff + a1 - a0], r=["zb"], w=["cst"])
                    i0 = sg[0][0]
                    zlo = 2 * (i0 + 1) + max(sg[0][1], c0) - c0
                    zhi = 2 * (sg[-1][0] + 1) + min(sg[-1][1] + sg[-1][2], c1) - c0
                    L = zhi - zlo
                    ts(OB[:, 0:L], ZB[:, zlo:zhi], WC[:, 2, c:c + 1], None, ALU.mult, None, r=["zb", "wc"], w=["ob"])
                    stt(OB[:, 0:L], ZB[:, zlo - 1:zhi - 1], WC[:, 1, c:c + 1], OB[:, 0:L], ALU.mult, ALU.add, r=["zb", "wc", "ob"], w=["ob"])
                    stt(OB[:, 0:L], ZB[:, zlo - 2:zhi - 2], WC[:, 0, c:c + 1], OB[:, 0:L], ALU.mult, ALU.add, r=["zb", "wc", "ob"], w=["ob"])
                    for (i_, s0, ln, kd) in sg:
                        a0, a1 = max(s0, c0), min(s0 + ln, c1)
                        zoff = 2 * (i_ + 1) + a0 - c0
                        tt(ZCg[:, c, a0:a1], OB[:, zoff - zlo:zoff - zlo + a1 - a0], PS[pc][:, a0 - c0:a1 - c0], ALU.mult, r=["ob", "ps%d" % pc], w=["zc"])
            flush()
            S.barrier()
            for c in range(8):
                sl5 = [(5 * c + i_) % NWS for i_ in range(5)]
                srcs = [(w_co, c * 128), (w_in, 4096 + c * 128), (w_glu, c * 128), (w_glu, 1024 + c * 128), (w_in, 5120 + c * 128)]
                for i_, (wd_, c0_) in enumerate(srcs):
                    S.dma("pool", WS[sl5[i_]], colblock(wd_, c0_), w=["ws%d" % sl5[i_]])
                for (c0, c1) in tls:
                    n = c1 - c0
                    pa, pb_ = psget(), psget()
                    mm_group([(PS[pa][:, 0:n], WS[sl5[0]][:, k, :], ZCg[:, k, c0:c1], k == 0, k == 7, None) for k in range(8)], r=["ws%d" % sl5[0], "zc"], w=["ps%d" % pa])
                    mm_group([(PS[pb_][:, 0:n], WS[sl5[1]][:, k, :], HTg[:, k, c0:c1], k == 0, k == 7, None) for k in range(8)], r=["ws%d" % sl5[1], htk], w=["ps%d" % pb_])
                    act(SGC[:, 0:n], PS[pb_][:, 0:n], AF.Sigmoid, r=["ps%d" % pb_], w=["sgc"])
                    tt(M1[:, 0:n], SGC[:, 0:n], PS[pa][:, 0:n], ALU.mult, r=["sgc", "ps%d" % pa], w=["m1"])
                    pc, pd, pe_ = psget(), psget(), psget()
                    mm_group([(PS[pc][:, 0:n], WS[sl5[2]][:, k, :], UT[:, k, tok0 + c0:tok0 + c1], k == 0, k == 7, None) for k in range(8)], r=["ws%d" % sl5[2], "ut"], w=["ps%d" % pc])
                    mm_group([(PS[pd][:, 0:n], WS[sl5[3]][:, k, :], UT[:, k, tok0 + c0:tok0 + c1], k == 0, k == 7, None) for k in range(8)], r=["ws%d" % sl5[3], "ut"], w=["ps%d" % pd])
                    mm_group([(PS[pe_][:, 0:n], WS[sl5[4]][:, k, :], HTg[:, k, c0:c1], k == 0, k == 7, None) for k in range(8)], r=["ws%d" % sl5[4], htk], w=["ps%d" % pe_])
                    act(SB_[:, 0:n], PS[pd][:, 0:n], AF.Sigmoid, r=["ps%d" % pd], w=["sb"])
                    act(SGS[:, 0:n], PS[pe_][:, 0:n], AF.Sigmoid, r=["ps%d" % pe_], w=["sgs"])
                    tt(TT_[:, 0:n], SB_[:, 0:n], PS[pc][:, 0:n], ALU.mult, r=["sb", "ps%d" % pc], w=["tt"])
                    tt(TT_[:, 0:n], TT_[:, 0:n], SGS[:, 0:n], ALU.mult, r=["tt", "sgs"], w=["tt"])
                    tt(MRG[:, c, c0:c1], TT_[:, 0:n], M1[:, 0:n], ALU.add, r=["tt", "m1"], w=["mrg"])
            for qd in range(4):
                WOB = WOBS[qd % 2]
                S.dma("pool", WOB, w_o[:, qd * 256:(qd + 1) * 256].rearrange("(k p) m -> p k m", p=128), w=["wob%d" % (qd % 2)])
                for b in range(nb):
                    pi = psget()
                    mm_group([(PS[pi][:, 0:256], MRG[:, k, b * 128:(b + 1) * 128], WOB[:, k, :], k == 0, k == 7, None) for k in range(8)],
                             r=["mrg", "wob%d" % (qd % 2)], w=["ps%d" % pi])
                    xs = xb(b)[:, qd * 256:(qd + 1) * 256]
                    tt(xs, xs, PS[pi][:, 0:256], ALU.add, r=["ps%d" % pi, xk(b)], w=[xk(b)])
            S.barrier()
            load_gain(0, g_ffn2)
            norm_T(xb, nb, 0, XNT, "xnt", xk)
            for half in range(2):
                hooks = None
                if half == 1 and not last:
                    nb0_, nb1_ = OWN_GROUPS[gi + 1]
                    nbn = nb1_ - nb0_
                    xbn = lambda b, b0_=nb0_: XRES[:, b0_ + b, :]
                    xkn = lambda b, b0_=nb0_: "x%d" % (b0_ + b)
                    hooks = {1: (lambda xbn=xbn, xkn=xkn, nbn=nbn: norm_stats(xbn, nbn, xkn, SS2, RSTD2, "ss2", "rstd2")),
                             6: (lambda xbn=xbn, xkn=xkn, nbn=nbn: norm_apply(xbn, nbn, 1, HT1, "xn1", xkn, RSTD2, "rstd2"))}
                ffn_gu(half, XNT, "xnt", nb, w2g, w2u, w2d, 0, hooks)
                ffn_dn(half, xb, nb, xk, w2d)
            if last:
                final_norm(gi)
            S.barrier()
        COUT = W(0, 4096, F32)
        pi = psget()
        tr_group([(PS[pi][0:10, c * 128:(c + 1) * 128], CST[:, c, :], IDF) for c in range(4)], r=["cst", "idf"], w=["ps%d" % pi])
        cp(COUT[0:10, 0:512], PS[pi][0:10, :], r=["ps%d" % pi], w=["cout"])
        pi = psget()
        tr_group([(PS[pi][0:10, c * 128:(c + 1) * 128], CST[:, 4 + c, :], IDF) for c in range(4)], r=["cst", "idf"], w=["ps%d" % pi])
        cp(COUT[0:10, 512:1024], PS[pi][0:10, :], r=["ps%d" % pi], w=["cout"])
        S.dma("sp", conv_o, COUT[0:10, :], r=["cout"])
        S.barrier()
        S.emit(block)
    return nc


_NC_CACHE = {}


def kernel(**inp):
    f32 = np.float32
    a = {k: np.asarray(v, dtype=f32) for k, v in inp.items()}
    if "nc" not in _NC_CACHE:
        _NC_CACHE["nc"] = build_program()
    nc = _NC_CACHE["nc"]
    ident = np.eye(128, dtype=f32)
    qmask = np.kron(np.eye(4, dtype=f32), np.ones((32, 32), dtype=f32))
    shared = {
        "norm_ffn1": a["norm_ffn1"].reshape(1, D), "norm_mix": a["norm_mix"].reshape(1, D),
        "norm_ffn2": a["norm_ffn2"].reshape(1, D), "norm_final": a["norm_final"].reshape(1, D),
        "w_ffn1_gate": a["w_ffn1_gate"][0], "w_ffn1_up": a["w_ffn1_up"][0], "w_ffn1_down": a["w_ffn1_down"][0],
        "w_ffn2_gate": a["w_ffn2_gate"][0], "w_ffn2_up": a["w_ffn2_up"][0], "w_ffn2_down": a["w_ffn2_down"][0],
        "w_in": a["w_in"][0], "w_conv": a["w_conv"][0], "w_conv_out": a["w_conv_out"][0],
        "ssm_lambda_re": a["ssm_lambda_re"][0], "ssm_lambda_im": a["ssm_lambda_im"][0], "ssm_log_step": a["ssm_log_step"].reshape(1, 64),
        "ssm_b_re": a["ssm_b_re"][0], "ssm_b_im": a["ssm_b_im"][0], "ssm_c_re": a["ssm_c_re"][0], "ssm_c_im": a["ssm_c_im"][0],
        "ssm_d": a["ssm_d"].reshape(1, D), "w_glu": a["w_glu"][0], "w_o": a["w_o"][0],
        "ident": ident, "qmask": qmask,
    }
    shared = {k: np.ascontiguousarray(v) for k, v in shared.items()}
    xp, xs = a["x_prompt"], a["x_sample"]
    in_maps = []
    for c in range(8):
        b, h = c // 2, c % 2
        m = dict(shared)
        m["xown"] = np.ascontiguousarray(np.concatenate([xp[b, h * 2048:(h + 1) * 2048], xs[4 * c:4 * c + 4].reshape(128, D)], axis=0))
        m["xpre"] = np.ascontiguousarray(xp[b, 0:2048]) if h == 1 else np.zeros((2048, D), f32)
        m["sconv"] = np.ascontiguousarray(a["state_conv"][0, 4 * c:4 * c + 4].reshape(8, D))
        m["sre"] = np.ascontiguousarray(a["state_ssm_re"][0, 4 * c:4 * c + 4].reshape(128, 128))
        m["sim"] = np.ascontiguousarray(a["state_ssm_im"][0, 4 * c:4 * c + 4].reshape(128, 128))
        in_maps.append(m)
    res = run_bass_kernel_spmd(nc, in_maps, core_ids=list(range(8)))
    R = res.results
    y_prompt = np.zeros((4, 4096, D), f32); y_sample = np.zeros((32, 32, D), f32)
    conv_p = np.zeros((1, 4, 2, D), f32); conv_s = np.zeros((1, 32, 2, D), f32)
    rp = np.zeros((1, 4, 64, 64), f32); ip = np.zeros((1, 4, 64, 64), f32)
    rs = np.zeros((1, 32, 64, 64), f32); is_ = np.zeros((1, 32, 64, 64), f32)
    for c in range(8):
        b, h = c // 2, c % 2
        y = R[c]["y"]
        y_prompt[b, h * 2048:(h + 1) * 2048] = y[0:2048]
        y_sample[4 * c:4 * c + 4] = y[2048:].reshape(4, 32, D)
        co = R[c]["conv_o"].reshape(5, 2, D)
        hr = R[c]["hre_o"].reshape(5, 64, 64); hi = R[c]["him_o"].reshape(5, 64, 64)
        conv_s[0, 4 * c:4 * c + 4] = co[1:5]
        rs[0, 4 * c:4 * c + 4] = hr[1:5]; is_[0, 4 * c:4 * c + 4] = hi[1:5]
        if h == 1:
            conv_p[0, b] = co[0]; rp[0, b] = hr[0]; ip[0, b] = hi[0]
    return (y_prompt, y_sample, conv_p, rp, ip, conv_s, rs, is_)
```

```python
import numpy as np
import concourse.bass as bass
import concourse.mybir as mybir
from concourse.bass_utils import run_bass_kernel_spmd

F32 = mybir.dt.float32
BF16 = mybir.dt.bfloat16
U8 = mybir.dt.uint8
AF = mybir.ActivationFunctionType
ALU = mybir.AluOpType

D = 1024
DFF = 2816
NF = 22
T = 16
NBO, NTO, NCO = 17, 2176, 136
NBP, NTP, NCP = 16, 2048, 128
OWN_GROUPS = [(0, 5), (5, 9), (9, 13), (13, 17)]
PRE_GROUPS = [(0, 4), (4, 8), (8, 12), (12, 16)]
PI = float(np.pi)


class Sched:
    ENG = ("pe", "act", "dve", "pool", "sp")

    def __init__(self, nc, sems, dsems):
        self.nc = nc
        self.sem = sems
        self.dsem = dsems
        self.ops = {e: [] for e in self.ENG}
        self.cnt = {e: 0 for e in self.ENG}
        self.dcnt = {q: [0] * len(dsems[q]) for q in dsems}
        self.drr = {q: 0 for q in dsems}
        self.seen = {e: {} for e in self.ENG}
        self.lw = {}
        self.rd = {}
        self.semobj = {}
        for e, s in sems.items():
            self.semobj[("c", e)] = s
        for q, l in dsems.items():
            for i, s in enumerate(l):
                self.semobj[("d", q, i)] = s

    def _deps(self, eng, r, w):
        need = {}
        for k in list(r) + list(w):
            if k in self.lw:
                s, v = self.lw[k]
                need[s] = max(need.get(s, 0), v)
        for k in w:
            for s, v in self.rd.get(k, {}).items():
                need[s] = max(need.get(s, 0), v)
        waits = []
        for s, v in need.items():
            if eng == "pe" and s == ("c", "pe"):
                continue
            if self.seen[eng].get(s, 0) >= v:
                continue
            self.seen[eng][s] = v
            waits.append((s, v))
        return waits

    def _commit(self, r, w, comp):
        s, v = comp
        for k in w:
            self.lw[k] = comp
            self.rd[k] = {}
        for k in r:
            d = self.rd.setdefault(k, {})
            d[s] = max(d.get(s, 0), v)

    defer = None

    def op(self, eng, fn, r=(), w=()):
        if self.defer is not None:
            self.defer.append(lambda: self._op(eng, fn, r, w))
            return
        self._op(eng, fn, r, w)

    def _op(self, eng, fn, r=(), w=()):
        waits = self._deps(eng, r, w)
        self.cnt[eng] += 1
        comp = (("c", eng), self.cnt[eng])
        self.ops[eng].append((waits, fn, comp, 1))
        self._commit(r, w, comp)

    defer_dma = None

    def dma(self, q, out, in_, r=(), w=(), **kw):
        if self.defer_dma is not None:
            self.defer_dma.append(lambda: self._dma(q, out, in_, r, w, **kw))
            return
        if self.defer is not None:
            self.defer.append(lambda: self._dma(q, out, in_, r, w, **kw))
            return
        self._dma(q, out, in_, r, w, **kw)

    def _dma(self, q, out, in_, r=(), w=(), **kw):
        i = self.drr[q]
        self.drr[q] = (i + 1) % len(self.dsem[q])
        s = ("d", q, i)
        waits = self._deps(q, r, w)
        prev = self.dcnt[q][i]
        if prev > 0 and self.seen[q].get(s, 0) < prev:
            self.seen[q][s] = prev
            waits.append((s, prev))
        self.dcnt[q][i] += 16
        comp = (s, self.dcnt[q][i])
        self.ops[q].append((waits, (lambda e, o=out, i_=in_, kw=kw: e.dma_start(out=o, in_=i_, **kw)), comp, 16))
        self._commit(r, w, comp)

    def barrier(self):
        allc = [(("c", e), self.cnt[e]) for e in ("pe", "act", "dve", "pool") if self.cnt[e] > 0]
        for q in self.dsem:
            for i, v in enumerate(self.dcnt[q]):
                if v > 0:
                    allc.append((("d", q, i), v))
        for e in self.ENG:
            waits = []
            for s, v in allc:
                if s == ("c", e):
                    continue
                if self.seen[e].get(s, 0) < v:
                    self.seen[e][s] = v
                    waits.append((s, v))
            if waits:
                self.ops[e].append((waits, None, None, 0))
        self.lw = {}
        self.rd = {}

    def emit(self, block):
        nc = self.nc

        def run(eng_name):
            def body(e):
                for waits, fn, comp, inc in self.ops[eng_name]:
                    for s, v in waits:
                        e.wait_ge(self.semobj[s], v)
                    if fn is not None:
                        ins = fn(e)
                        ins.then_inc(self.semobj[comp[0]], inc)
                if eng_name in self.dsem:
                    for i, v in enumerate(self.dcnt[eng_name]):
                        if v > 0:
                            e.wait_ge(self.semobj[("d", eng_name, i)], v)
            return body

        block.tensor(run("pe"))
        block.scalar(run("act"))
        block.vector(run("dve"))
        block.gpsimd(run("pool"))
        block.sync(run("sp"))


def tiles_of(nb):
    if nb <= 4:
        return [(0, nb * 128)]
    h = (nb + 1) // 2
    return [(0, h * 128), (h * 128, nb * 128)]


def build_program():
    nc = bass.Bass("TRN2", target_bir_lowering=False)

    def din(name, shape):
        return nc.dram_tensor(name, list(shape), F32, kind="ExternalInput").ap()

    def dout(name, shape):
        return nc.dram_tensor(name, list(shape), F32, kind="ExternalOutput").ap()

    xown = din("xown", [NTO, D]); xpre = din("xpre", [NTP, D])
    sconv = din("sconv", [8, D]); sre = din("sre", [128, 128]); sim = din("sim", [128, 128])
    g_ffn1 = din("norm_ffn1", [1, D]); g_mix = din("norm_mix", [1, D]); g_ffn2 = din("norm_ffn2", [1, D]); g_fin = din("norm_final", [1, D])
    w1g = din("w_ffn1_gate", [D, DFF]); w1u = din("w_ffn1_up", [D, DFF]); w1d = din("w_ffn1_down", [DFF, D])
    w2g = din("w_ffn2_gate", [D, DFF]); w2u = din("w_ffn2_up", [D, DFF]); w2d = din("w_ffn2_down", [DFF, D])
    w_in = din("w_in", [D, 6144]); w_conv = din("w_conv", [3, D]); w_co = din("w_conv_out", [D, D])
    lam_re = din("ssm_lambda_re", [64, 64]); lam_im = din("ssm_lambda_im", [64, 64]); lstep = din("ssm_log_step", [1, 64])
    b_re = din("ssm_b_re", [64, 64, 16]); b_im = din("ssm_b_im", [64, 64, 16])
    c_re = din("ssm_c_re", [64, 16, 64]); c_im = din("ssm_c_im", [64, 16, 64])
    ssm_d = din("ssm_d", [1, D]); w_glu = din("w_glu", [D, 2 * D]); w_o = din("w_o", [D, D])
    identd = din("ident", [128, 128]); maskd = din("qmask", [128, 128])
    y_o = dout("y", [NTO, D]); conv_o = dout("conv_o", [10, D]); hre_o = dout("hre_o", [5 * 32, 128]); him_o = dout("him_o", [5 * 32, 128])

    scr_wit = nc.dram_tensor("scr_wit", [8, 128, 4096], BF16).ap()
    scr_wbd = nc.dram_tensor("scr_wbd", [8, 128, 4096], BF16).ap()
    scr_g = nc.dram_tensor("scr_g", [22, 128, 1024], BF16).ap()
    scr_u = nc.dram_tensor("scr_u", [22, 128, 1024], BF16).ap()
    scr_d = nc.dram_tensor("scr_d", [8, 128, 2816], BF16).ap()
    scr_up = nc.dram_tensor("scr_up", [8, 128, 1024], BF16).ap()
    ARENA = 210944
    import contextlib
    with contextlib.ExitStack() as es:
        arena = es.enter_context(nc.sbuf_tensor("arena", [128, ARENA], U8))
        PS = [es.enter_context(nc.psum_tensor("ps%d" % i, [128, 512], F32)) for i in range(8)]
        s_pe, s_act, s_dve, s_pool = [es.enter_context(nc.semaphore(n)) for n in ("s_pe", "s_act", "s_dve", "s_pool")]
        dps = [es.enter_context(nc.semaphore("dp%d" % i)) for i in range(8)]
        dss = [es.enter_context(nc.semaphore("ds%d" % i)) for i in range(8)]
        block = es.enter_context(nc.Block())
        S = Sched(nc, {"pe": s_pe, "act": s_act, "dve": s_dve, "pool": s_pool},
                  {"pool": dps, "sp": dss})

        off = [0]

        def carve(nbytes, dt, pattern=None, **kw):
            a = arena[:, off[0]:off[0] + nbytes].bitcast(dt)
            off[0] += (nbytes + 63) // 64 * 64
            assert off[0] <= ARENA, off[0]
            if pattern:
                a = a.rearrange(pattern, **kw)
            return a

        def carve_at(off_, nbytes, dt, pattern=None, **kw):
            assert off_ + nbytes <= ARENA, (off_, nbytes)
            a = arena[:, off_:off_ + nbytes].bitcast(dt)
            if pattern:
                a = a.rearrange(pattern, **kw)
            return a

        XRES = carve(NBO * D * 4, F32, "p (b d) -> p b d", d=D)
        GBC = carve(2 * D * 4, F32, "p (s d) -> p s d", d=D)
        UT = carve(8 * NTO * 2, BF16, "p (c t) -> p c t", t=NTO)
        UTP = UT.rearrange("p c (j k) -> p c j k", j=T)
        IDB = carve(128 * 2, BF16); IDF = carve(128 * 4, F32); MSK = carve(128 * 4, F32)
        WC = carve(3 * 8 * 4, F32, "p (k c) -> p k c", c=8); DV = carve(8 * 4, F32)
        SS = carve(32 * 4, F32); RSTD = carve(32 * 4, F32)
        BNr = carve(32 * 16 * 4, F32, "p (r c) -> p r c", c=16); BNi = carve(32 * 16 * 4, F32, "p (r c) -> p r c", c=16)
        CSr = carve(32 * 32 * 4, F32, "p (r c) -> p r c", c=32); CSi = carve(32 * 32 * 4, F32, "p (r c) -> p r c", c=32)
        CBr = carve(32 * 32 * 2, BF16, "p (r c) -> p r c", c=32); CBni = carve(32 * 32 * 2, BF16, "p (r c) -> p r c", c=32)
        PWr = carve(32 * 17 * 4, F32, "p (r j) -> p r j", j=17); PWi = carve(32 * 17 * 4, F32, "p (r j) -> p r j", j=17)
        SQr = carve(32 * 16 * 4, F32, "p (r j) -> p r j", j=16); SQi = carve(32 * 16 * 4, F32, "p (r j) -> p r j", j=16)
        tl = [carve(32 * 4, F32) for _ in range(14)]
        EPSB = carve(4 * 4, F32)
        COEF = carve(2 * 2 * 32 * 4, F32, "p (o i r) -> p o i r", o=2, i=2)
        SA = [carve(2 * 32 * 4, F32, "p (i r) -> p i r", i=2) for _ in range(2)]
        TPA = [carve(2 * 3 * 32 * 4, F32, "p (o i r) -> p o i r", o=2, i=3) for _ in range(2)]
        SS2 = carve(32 * 4, F32); RSTD2 = carve(32 * 4, F32)
        HF = carve(5 * 2 * 32 * 4, F32, "p (s i r) -> p s i r", s=5, i=2)
        ZPREV = carve(8 * 2 * 4, F32, "p (c k) -> p c k", k=2)
        SCT = carve(8 * 8 * 4, F32, "p (c k) -> p c k", k=8)
        CST = carve(8 * 10 * 4, F32, "p (c k) -> p c k", k=10)
        WORK0 = off[0]
        WORKSZ = ARENA - WORK0

        def W(off_, nbytes, dt, pattern=None, **kw):
            assert off_ + nbytes <= WORKSZ, (off_, nbytes, WORKSZ)
            return carve_at(WORK0 + off_, nbytes, dt, pattern, **kw)

        XN = [W(0, 8 * 640 * 2, BF16, "p (c t) -> p c t", t=640), W(10240, 8 * 512 * 2, BF16, "p (c t) -> p c t", t=512)]
        XNT = XN[0]
        SIL = [W(18432, 2048, F32)]
        XSBS = [W(20480, 2048, BF16), W(22528, 2048, BF16)]
        JNK = W(24576, 2048, BF16)
        HID = W(26624, 11 * 640 * 2, BF16, "p (f t) -> p f t", t=640)
        WGU = [W(40704 + i * 4096, 4096, BF16, "p (k g m) -> p k g m", g=2, m=128) for i in range(3)]
        WDS = [W(52992 + i * 5632, 5632, BF16, "p (f m) -> p f m", m=256) for i in range(2)]
        WBD = W(0, 8192, BF16, "p (j r m) -> p j r m", r=2, m=128)
        WIT = W(8192, 8192, BF16, "p (n m) -> p n m", m=128)
        TA = W(16384, 4096, F32); TB = W(20480, 4096, F32)
        XH = W(24576, 32 * 2 * NCO * 4, F32, "p (i r k) -> p i r k", i=2, k=NCO)
        XHP = XRES[:, 9:17, :].rearrange("p b d -> p (b d)").rearrange("p (i r k) -> p i r k", i=2, k=NCP)
        HB = W(0, 32 * 2 * NCO * 2, BF16, "p (i r k) -> p i r k", i=2, k=NCO)
        S4 = [W(17408 + i * 1024, 1024, F32, "p (i r s) -> p i r s", i=2, s=4) for i in range(2)]
        TP4A = [W(19456, 3072, F32, "p (o i r s) -> p o i r s", o=2, i=3, s=4)]
        WBD2 = W(17408, 8192, BF16, "p (j r m) -> p j r m", r=2, m=128)
        TA2 = W(25600, 4096, F32); TB2 = W(29696, 4096, F32)
        HT = W(0, 8 * 640 * 2, BF16, "p (c t) -> p c t", t=640)


        ident_bf = IDB
        psrr = [0]

        def psget():
            i = psrr[0]
            psrr[0] = (i + 1) % 8
            return i

        def mm_group(lst, r, w):
            def fn(e, lst=lst):
                ins = None
                for (o, l, rh, st, sp_, tp) in lst:
                    if tp is None:
                        ins = e.matmul(o, lhsT=l, rhs=rh, start=st, stop=sp_)
                    else:
                        ins = e.matmul(o, lhsT=l, rhs=rh, start=st, stop=sp_, tile_position=tp)
                return ins
            S.op("pe", fn, r=r, w=w)

        def tr_group(lst, r, w):
            def fn(e, lst=lst):
                ins = None
                for (o, i_, idt) in lst:
                    ins = e.transpose(o, i_, idt)
                return ins
            S.op("pe", fn, r=r, w=w)

        def act(out, in_, func, r, w, **kw):
            S.op("act", lambda e: e.activation(out=out, in_=in_, func=func, **kw), r=r, w=w)

        def tt(out, in0, in1, op, r, w, eng="dve"):
            S.op(eng, lambda e: e.tensor_tensor(out=out, in0=in0, in1=in1, op=op), r=r, w=w)

        def ts(out, in0, s1, s2, op0, op1, r, w, eng="dve"):
            if s2 is None:
                S.op(eng, lambda e: e.tensor_scalar(out=out, in0=in0, scalar1=s1, scalar2=None, op0=op0), r=r, w=w)
            else:
                S.op(eng, lambda e: e.tensor_scalar(out=out, in0=in0, scalar1=s1, scalar2=s2, op0=op0, op1=op1), r=r, w=w)

        def stt(out, in0, scalar, in1, op0, op1, r, w, eng="dve"):
            S.op(eng, lambda e: e.scalar_tensor_tensor(out=out, in0=in0, scalar=scalar, in1=in1, op0=op0, op1=op1), r=r, w=w)

        def cp(out, in_, r, w, eng="dve"):
            S.op(eng, lambda e: e.tensor_copy(out=out, in_=in_), r=r, w=w)

        def ms(ap, val, w, eng="dve"):
            S.op(eng, lambda e: e.memset(ap, val), r=(), w=w)

        def colblock(Wd_, c0, n=128):
            return Wd_[:, c0:c0 + n].rearrange("(k p) m -> p k m", p=128)

        bg = []

        def pump(n=1):
            for _ in range(n):
                if bg:
                    bg.pop(0)()

        def flush():
            while bg:
                bg.pop(0)()

        def atomic(f):
            if S.defer is None:
                f()
                return
            saved = S.defer
            tmp = []
            S.defer = tmp
            try:
                f()
            finally:
                S.defer = saved
            saved.append(lambda tmp=tmp: [t() for t in tmp])

        S.dma("pool", IDB, identd, w=["idb"])
        ms(EPSB, 1e-6, ["epsb"])
        bg_dma = []
        S.defer = bg
        S.defer_dma = bg_dma
        S.dma("sp", IDF, identd, w=["idf"])
        S.dma("sp", MSK, maskd, w=["msk"])
        with nc.allow_non_contiguous_dma(reason="tiny param loads"):
            pass
        S.dma("sp", WC, w_conv.rearrange("k (c p) -> p k c", p=128), w=["wc"], allow_slow_non_contiguous=True)
        S.dma("sp", DV, ssm_d.rearrange("o (c p) -> p (o c)", p=128), w=["dv"], allow_slow_non_contiguous=True)
        LRE, LIM, LS = tl[0], tl[1], tl[2]
        LAMT = XRES[0:32, 8, 0:384].rearrange("p (i m) -> p i m", m=128)
        LSS = XRES[0:32, 8, 384:386]
        S.dma("sp", LAMT[:, 0, :], lam_re.rearrange("(r e) p -> r (e p)", e=2), w=["lamt0"])
        S.dma("sp", LAMT[:, 1, :], lam_im.rearrange("(r e) p -> r (e p)", e=2), w=["lamt1"])
        S.dma("sp", LSS, lstep.rearrange("o (r e) -> r (o e)", e=2), w=["lss"], allow_slow_non_contiguous=True)
        cp(LAMT[:, 2, :].rearrange("p (e q) -> p e q", e=2), LSS.unsqueeze(2).to_broadcast([32, 2, 64]), r=["lss"], w=["lamt2"])

        def _lamtr():
            pi = psget()
            tr_group([(PS[pi][:, i_ * 32:(i_ + 1) * 32], LAMT[0:32, i_, :], IDF[0:32, 0:32]) for i_ in range(3)],
                     r=["lamt0", "lamt1", "lamt2", "idf"], w=["ps%d" % pi])
            cp(LRE, PS[pi][:, 0:32], r=["ps%d" % pi], w=["lre"])
            cp(LIM, PS[pi][:, 32:64], r=["ps%d" % pi], w=["lim"])
            cp(LS, PS[pi][:, 64:96], r=["ps%d" % pi], w=["ls"])
        atomic(_lamtr)
        for (BN_, bd_, blk0, kk_) in ((BNr, b_re, 9, "bnr"), (BNi, b_im, 13, "bni")):
            BT = XRES[32:64, blk0:blk0 + 2, :].rearrange("p b d -> p (b d)")
            S.dma("sp", BT, bd_.rearrange("(r e) p c -> r (e p c)", e=2), w=[kk_ + "t"])

            def _bntr(BN_=BN_, BT=BT, kk_=kk_):
                pi = psget()
                btv = BT.rearrange("p (m c) -> p c m", c=16)
                tr_group([(PS[pi][:, c_ * 32:(c_ + 1) * 32], btv[:, c_, :], IDF[32:64, 32:64]) for c_ in range(16)],
                         r=[kk_ + "t", "idf"], w=["ps%d" % pi])
                cp(BN_, PS[pi][:].rearrange("p (c r) -> p r c", c=16), r=["ps%d" % pi], w=[kk_])
            atomic(_bntr)

        CTr = XRES[:, 9:13, :].rearrange("p b d -> p (b d)").rearrange("p (r m) -> p r m", m=128)
        CTi = XRES[:, 13:17, :].rearrange("p b d -> p (b d)").rearrange("p (r m) -> p r m", m=128)
        S.defer = None
        ms(CTr[0:32], 0.0, ["ctr"]); ms(CTi[0:32], 0.0, ["cti"])
        S.defer = bg
        for (CT, cd, k) in ((CTr, c_re, "ctr"), (CTi, c_im, "cti")):
            cv = cd.rearrange("(r e) c p -> e c r p", e=2)
            for e_ in range(2):
                S.dma("sp", CT[16 * e_:16 * e_ + 16, :, 64 * e_:64 * e_ + 64], cv[e_], r=[], w=[k])
        for (CT, CS, k, ks) in ((CTr, CSr, "ctr", "csr"), (CTi, CSi, "cti", "csi")):
            for hb in range(2):
                def _ct(CT=CT, CS=CS, k=k, ks=ks, hb=hb):
                    pi = psget()
                    tr_group([(PS[pi][:, r_ * 32:(r_ + 1) * 32], CT[0:32, hb * 16 + r_, :], IDF[0:32, 0:32]) for r_ in range(16)],
                             r=[k, "idf"], w=["ps%d" % pi])
                    cp(CS[:, hb * 16:(hb + 1) * 16, :], PS[pi][:].rearrange("p (r c) -> p r c", c=32), r=["ps%d" % pi], w=[ks])
                atomic(_ct)
        cp(CBr, CSr, r=["csr"], w=["cbr"])
        ts(CBni, CSi, -1.0, None, ALU.mult, None, r=["csi"], w=["cbni"])

        DT, MAG, PH, PHC, SN, CSN, LR, LI, QR, QI, TM, TM2 = tl[3], tl[4], tl[5], tl[6], tl[7], tl[8], tl[9], tl[10], tl[11], tl[12], tl[13], tl[2]
        act(DT, LS, AF.Exp, r=["ls"], w=["dt"])
        tt(MAG, LRE, DT, ALU.mult, r=["lre", "dt"], w=["mag"])
        act(MAG, MAG, AF.Exp, r=["mag"], w=["mag"])
        tt(PH, LIM, DT, ALU.mult, r=["lim", "dt"], w=["ph"])
        ts(PHC, PH, PI / 2, None, ALU.add, None, r=["ph"], w=["phc"])
        for (P_, k) in ((PH, "ph"), (PHC, "phc")):
            for it in range(8):
                ts(TM, P_, PI, -2 * PI, ALU.is_gt, ALU.mult, r=[k], w=["tm"])
                tt(P_, P_, TM, ALU.add, r=[k, "tm"], w=[k])
        act(SN, PH, AF.Sin, r=["ph"], w=["sn"])
        act(CSN, PHC, AF.Sin, r=["phc"], w=["csn"])
        tt(LR, MAG, CSN, ALU.mult, r=["mag", "csn"], w=["lr"])
        tt(LI, MAG, SN, ALU.mult, r=["mag", "sn"], w=["li"])
        NR = DT
        ts(NR, LR, -1.0, None, ALU.add, None, r=["lr", "mag"], w=["dt"])
        DEN = MAG
        tt(DEN, LRE, LRE, ALU.mult, r=["lre", "lr", "li"], w=["mag"])
        tt(TM, LIM, LIM, ALU.mult, r=["lim"], w=["tm"])
        tt(DEN, DEN, TM, ALU.add, r=["mag", "tm"], w=["mag"])
        S.op("dve", lambda e: e.reciprocal(out=DEN, in_=DEN), r=["mag"], w=["mag"])
        tt(QR, NR, LRE, ALU.mult, r=["dt", "lre"], w=["qr"])
        tt(TM, LI, LIM, ALU.mult, r=["li", "lim"], w=["tm"])
        tt(QR, QR, TM, ALU.add, r=["qr", "tm"], w=["qr"])
        tt(QR, QR, DEN, ALU.mult, r=["qr", "mag"], w=["qr"])
        tt(QI, LI, LRE, ALU.mult, r=["li", "lre"], w=["qi"])
        tt(TM, NR, LIM, ALU.mult, r=["dt", "lim"], w=["tm"])
        tt(QI, QI, TM, ALU.subtract, r=["qi", "tm"], w=["qi"])
        tt(QI, QI, DEN, ALU.mult, r=["qi", "mag"], w=["qi"])
        ms(PWr[:, :, 0], 1.0, ["pw"]); ms(PWi[:, :, 0], 0.0, ["pw"])
        for j in range(16):
            tt(TM, PWr[:, :, j], LR, ALU.mult, r=["pw", "lr"], w=["tm"])
            tt(SN, PWi[:, :, j], LI, ALU.mult, r=["pw", "li"], w=["sn"])
            tt(PWr[:, :, j + 1], TM, SN, ALU.subtract, r=["tm", "sn"], w=["pw"])
            tt(TM, PWr[:, :, j], LI, ALU.mult, r=["pw", "li"], w=["tm"])
            tt(SN, PWi[:, :, j], LR, ALU.mult, r=["pw", "lr"], w=["sn"])
            tt(PWi[:, :, j + 1], TM, SN, ALU.add, r=["tm", "sn"], w=["pw"])
        AR, AI = CSN, PHC
        cp(AR, PWr[:, :, 16], r=["pw"], w=["ar"]); cp(AI, PWi[:, :, 16], r=["pw"], w=["ai"])
        for j in range(16):
            jj = T - 1 - j
            tt(TM, PWr[:, :, jj], QR, ALU.mult, r=["pw", "qr"], w=["tm"])
            tt(SN, PWi[:, :, jj], QI, ALU.mult, r=["pw", "qi"], w=["sn"])
            tt(SQr[:, :, j], TM, SN, ALU.subtract, r=["tm", "sn"], w=["sq"])
            tt(TM, PWr[:, :, jj], QI, ALU.mult, r=["pw", "qi"], w=["tm"])
            tt(SN, PWi[:, :, jj], QR, ALU.mult, r=["pw", "qr"], w=["sn"])
            tt(SQi[:, :, j], TM, SN, ALU.add, r=["tm", "sn"], w=["sq"])
        cp(COEF[:, 0, 0, :], AR, r=["ar"], w=["coef"]); cp(COEF[:, 1, 1, :], AR, r=["ar"], w=["coef"])
        cp(COEF[:, 1, 0, :], AI, r=["ai"], w=["coef"]); ts(COEF[:, 0, 1, :], AI, -1.0, None, ALU.mult, None, r=["ai"], w=["coef"])
        S.defer = None
        S.defer_dma = None

        def load_gain(slot, gd):
            S.dma("sp", GBC[:, slot, :], gd[0, :].partition_broadcast(128), w=["gbc%d" % slot])

        def norm_stats(xblk, nb, xkeys, SSx=SS, RSTDx=RSTD, sk="ss", rk="rstd"):
            ms(SSx, 0.0, [sk])
            for b in range(nb):
                act(JNK, xblk(b), AF.Square, r=[xkeys(b)], w=["jnk", sk], accum_out=SSx[:, b:b + 1])
            act(RSTDx[:, 0:nb], SSx[:, 0:nb], AF.Sqrt, r=[sk, "epsb"], w=[rk], scale=1.0 / D, bias=EPSB[:, 0:1])
            S.op("dve", lambda e: e.reciprocal(out=RSTDx[:, 0:nb], in_=RSTDx[:, 0:nb]), r=[rk], w=[rk])

        def norm_apply(xblk, nb, slot, dst, dkey, xkeys, RSTDx=RSTD, rk="rstd"):
            for b in range(nb):
                XSB = XSBS[b % 2]
                xk_ = "xsb%d" % (b % 2)
                stt(XSB, xblk(b), RSTDx[:, b:b + 1], GBC[:, slot, :], ALU.mult, ALU.mult, r=[xkeys(b), rk, "gbc%d" % slot], w=[xk_])
                pi = psget()
                pb = PS[pi][:].bitcast(BF16)
                tr_group([(pb[:, c * 128:(c + 1) * 128], XSB[:, c * 128:(c + 1) * 128], IDB) for c in range(8)],
                         r=[xk_, "idb"], w=["ps%d" % pi])
                act(dst[:, :, b * 128:(b + 1) * 128], pb.rearrange("p (c t) -> p c t", t=128), AF.Copy, r=["ps%d" % pi], w=[dkey])

        def norm_T(xblk, nb, slot, dst, dkey, xkeys):
            norm_stats(xblk, nb, xkeys)
            norm_apply(xblk, nb, slot, dst, dkey, xkeys)

        def wd_dma(Wd, half, qd, wc=None):
            ws = (half * 4 + qd) % 2
            i8 = half * 4 + qd
            if wc is not None and not wc["first"]:
                S.dma("sp", WDS[ws].rearrange("p f m -> p (f m)"), scr_d[i8], r=["scrd%d" % i8], w=["wds%d" % ws])
                return
            S.dma("pool", WDS[ws], Wd[half * 1408:(half + 1) * 1408, qd * 256:(qd + 1) * 256].rearrange("(f p) m -> p f m", p=128), w=["wds%d" % ws])
            if wc is not None:
                S.dma("sp", scr_d[i8], WDS[ws].rearrange("p f m -> p (f m)"), r=["wds%d" % ws], w=["scrd%d" % i8])

        def ffn_gu(half, XNg, xnk, nb, Wg, Wu, Wd, npump=0, hooks=None, wc=None):
            tls = tiles_of(nb)
            for fi in range(11):
                f = half * 11 + fi
                if fi == 3:
                    wd_dma(Wd, half, 0, wc); wd_dma(Wd, half, 1, wc)
                sl = f % 3
                if wc is not None and not wc["first"]:
                    S.dma("sp", WGU[sl][:, :, 0, :], scr_g[f].rearrange("p (k m) -> p k m", m=128), r=["scrg%d" % f], w=["wgu%dg" % sl])
                    S.dma("sp", WGU[sl][:, :, 1, :], scr_u[f].rearrange("p (k m) -> p k m", m=128), r=["scru%d" % f], w=["wgu%du" % sl])
                else:
                    S.dma("pool", WGU[sl][:, :, 0, :], colblock(Wg, f * 128), w=["wgu%dg" % sl])
                    S.dma("pool", WGU[sl][:, :, 1, :], colblock(Wu, f * 128), w=["wgu%du" % sl])
                    if wc is not None:
                        S.dma("sp", scr_g[f].rearrange("p (k m) -> p k m", m=128), WGU[sl][:, :, 0, :], r=["wgu%dg" % sl], w=["scrg%d" % f])
                        S.dma("sp", scr_u[f].rearrange("p (k m) -> p k m", m=128), WGU[sl][:, :, 1, :], r=["wgu%du" % sl], w=["scru%d" % f])
                for ti, (c0, c1) in enumerate(tls):
                    pa, pb_ = psget(), psget()
                    n = c1 - c0
                    mm_group([(PS[pa][:, 0:n], WGU[sl][:, k, 0, :], XNg[:, k, c0:c1], k == 0, k == 7, None) for k in range(8)],
                             r=["wgu%dg" % sl, xnk], w=["ps%d" % pa])
                    mm_group([(PS[pb_][:, 0:n], WGU[sl][:, k, 1, :], XNg[:, k, c0:c1], k == 0, k == 7, None) for k in range(8)],
                             r=["wgu%du" % sl, xnk], w=["ps%d" % pb_])
                    act(SIL[0][:, 0:n], PS[pa][:, 0:n], AF.Silu, r=["ps%d" % pa], w=["sil0"])
                    tt(HID[:, fi, c0:c1], SIL[0][:, 0:n], PS[pb_][:, 0:n], ALU.mult, r=["sil0", "ps%d" % pb_], w=["hid"])
                pump(npump)
                if hooks and fi in hooks:
                    hooks[fi]()

        def ffn_dn(half, xblk, nb, xkeys, Wd, wc=None):
            for qd in range(4):
                ws = (half * 4 + qd) % 2
                if qd >= 2:
                    wd_dma(Wd, half, qd, wc)
                for b in range(nb):
                    pi = psget()
                    mm_group([(PS[pi][:, 0:256], HID[:, fi, b * 128:(b + 1) * 128], WDS[ws][:, fi, :], fi == 0, fi == 10, None) for fi in range(11)],
                             r=["hid", "wds%d" % ws], w=["ps%d" % pi])
                    xs = xblk(b)[:, qd * 256:(qd + 1) * 256]
                    stt(xs, PS[pi][:, 0:256], 0.5, xs, ALU.mult, ALU.add, r=["ps%d" % pi, xkeys(b)], w=[xkeys(b)])

        def ffn(xblk, nb, xkeys, gd, Wg, Wu, Wd, npump=0):
            load_gain(0, gd)
            norm_T(xblk, nb, 0, XNT, "xnt", xkeys)
            for half in range(2):
                ffn_gu(half, XNT, "xnt", nb, Wg, Wu, Wd, npump)
                ffn_dn(half, xblk, nb, xkeys, Wd)

        upc = [0]

        def uproj(nb, tok0, src, skey):
            tls = tiles_of(nb)
            first_up = upc[0] == 0
            upc[0] += 1
            for c in range(8):
                sl = c % 3
                if first_up:
                    S.dma("pool", WGU[sl][:, :, 0, :], colblock(w_in, 3072 + c * 128), w=["wgu%dg" % sl])
                    S.dma("sp", scr_up[c].rearrange("p (k m) -> p k m", m=128), WGU[sl][:, :, 0, :], r=["wgu%dg" % sl], w=["scrup%d" % c])
                else:
                    S.dma("sp", WGU[sl][:, :, 0, :], scr_up[c].rearrange("p (k m) -> p k m", m=128), r=["scrup%d" % c], w=["wgu%dg" % sl])
                for (c0, c1) in tls:
                    pi = psget()
                    n = c1 - c0
                    mm_group([(PS[pi][:, 0:n], WGU[sl][:, k, 0, :], src[:, k, c0:c1], k == 0, k == 7, None) for k in range(8)],
                             r=["wgu%dg" % sl, skey], w=["ps%d" % pi])
                    k0_, k1_ = (tok0 + c0) // T, (tok0 + c1) // T
                    act(UTP[:, c, :, k0_:k1_], PS[pi][:, 0:n].rearrange("p (k j) -> p j k", j=T), AF.Copy, r=["ps%d" % pi], w=["ut"])

        def phase1(groups, xb_of, xk_of, npump_of, hoist_ok, tail=None, wc_of=None):
            n = len(groups)
            nb0 = groups[0][1] - groups[0][0]
            norm_stats(xb_of(0), nb0, xk_of(0))
            norm_apply(xb_of(0), nb0, 0, XN[0], "xn0", xk_of(0))

            def NMs(g):
                nb = groups[g][1] - groups[g][0]
                norm_stats(xb_of(g), nb, xk_of(g), SS2, RSTD2, "ss2", "rstd2")

            def NMa(g):
                nb = groups[g][1] - groups[g][0]
                norm_apply(xb_of(g), nb, 1, XN[g % 2], "xn%d" % (g % 2), xk_of(g), RSTD2, "rstd2")

            def UP(g):
                nb = groups[g][1] - groups[g][0]
                uproj(nb, groups[g][0] * 128, XN[g % 2], "xn%d" % (g % 2))

            for g in range(n):
                nb = groups[g][1] - groups[g][0]
                XNg, xnk = XN[g % 2], "xn%d" % (g % 2)
                h0 = {2: (lambda g=g: NMa(g - 1))} if g > 0 else None
                wc = wc_of(g) if wc_of else None
                ffn_gu(0, XNg, xnk, nb, w1g, w1u, w1d, npump_of(g, 0), h0, wc)
                if g > 0:
                    UP(g - 1)
                ffn_dn(0, xb_of(g), nb, xk_of(g), w1d, wc)
                h1 = None
                hoisted = (g + 1 < n) and hoist_ok(g + 1)
                if hoisted:
                    nbn = groups[g + 1][1] - groups[g + 1][0]
                    h1 = {1: (lambda g=g, nbn=nbn: norm_stats(xb_of(g + 1), nbn, xk_of(g + 1))),
                          6: (lambda g=g, nbn=nbn: norm_apply(xb_of(g + 1), nbn, 0, XN[(g + 1) % 2], "xn%d" % ((g + 1) % 2), xk_of(g + 1)))}
                ffn_gu(1, XNg, xnk, nb, w1g, w1u, w1d, npump_of(g, 1), h1, wc)
                ffn_dn(1, xb_of(g), nb, xk_of(g), w1d, wc)
                NMs(g)
                if (g + 1 < n) and not hoisted:
                    if tail:
                        tail(g)
                    nbn = groups[g + 1][1] - groups[g + 1][0]
                    norm_stats(xb_of(g + 1), nbn, xk_of(g + 1))
                    norm_apply(xb_of(g + 1), nbn, 0, XN[(g + 1) % 2], "xn%d" % ((g + 1) % 2), xk_of(g + 1))
            NMa(n - 1)
            UP(n - 1)

        wslot = [0]
        WCOL_OFF = [None]

        def gen_wbd_part(ch, WBDx, ri, eng, TAx, TBx, nsplit, ka, kb, wpre="wbd"):
            nq = 4 // nsplit
            for qs in range(nsplit):
                p0_ = ch * 4 + qs * nq
                sr = SQr[:, p0_:p0_ + nq, :].unsqueeze(3).to_broadcast([128, nq, 16, 16])
                si = SQi[:, p0_:p0_ + nq, :].unsqueeze(3).to_broadcast([128, nq, 16, 16])
                br = BNr[:, p0_:p0_ + nq, :].unsqueeze(2).to_broadcast([128, nq, 16, 16])
                bi = BNi[:, p0_:p0_ + nq, :].unsqueeze(2).to_broadcast([128, nq, 16, 16])
                ta = TAx.rearrange("p (q j c) -> p q j c", q=nq, j=16)
                tb = TBx.rearrange("p (q j c) -> p q j c", q=nq, j=16)
                if ri == 0:
                    tt(ta, sr, br, ALU.mult, r=["sq", "bnr"], w=ka, eng=eng)
                    tt(tb, si, bi, ALU.mult, r=["sq", "bni"], w=kb, eng=eng)
                    op = ALU.subtract
                else:
                    tt(ta, sr, bi, ALU.mult, r=["sq", "bni"], w=ka, eng=eng)
                    tt(tb, si, br, ALU.mult, r=["sq", "bnr"], w=kb, eng=eng)
                    op = ALU.add
                for e_ in range(2):
                    o = WBDx[64 * e_:64 * e_ + 64, :, ri, :].rearrange("p j (q e c) -> p q j e c", q=4, e=2)[:, qs * nq:(qs + 1) * nq, :, e_, :]
                    tt(o, ta[64 * e_:64 * e_ + 64], tb[64 * e_:64 * e_ + 64], op, r=ka + kb, w=["%s%d" % (wpre, ri)], eng=eng)

        TAP = W(59392, 2048, F32); TBP = W(61440, 2048, F32)

        def gen_wbd(ch):
            gen_wbd_part(ch, WBD, 0, "dve", TA, TB, 1, ["ta"], ["tb"])
            gen_wbd_part(ch, WBD, 1, "pool", TAP, TBP, 2, ["tap"], ["tbp"])

        GX = XRES[:, 9:17, :].rearrange("p b d -> p (b d)")
        GWBDS = [GX[:, 0:2048].bitcast(BF16).rearrange("p (j r m) -> p j r m", r=2, m=128),
                 GX[:, 6144:8192].bitcast(BF16).rearrange("p (j r m) -> p j r m", r=2, m=128)]
        GWIT = GX[:, 2048:4096].bitcast(BF16).rearrange("p (n m) -> p n m", m=128)
        GTA = GX[:, 4096:4608]; GTB = GX[:, 4608:5120]; GTAP = GX[:, 5120:5632]; GTBP = GX[:, 5632:6144]

        def gen_all():
            ms(GX, 0.0, ["ga0", "ga1", "gb0", "gb1", "ctr", "cti", "wit0", "wit1", "wit2", "wit3", "ta", "tb", "tap", "tbp"], eng="pool")

            def elem(ch):
                G = GWBDS[ch % 2]
                wp = "ga" if ch % 2 == 0 else "gb"
                gen_wbd_part(ch, G, 0, "dve", GTA, GTB, 2, ["ta"], ["tb"], wpre=wp)
                gen_wbd_part(ch, G, 1, "pool", GTAP, GTBP, 2, ["tap"], ["tbp"], wpre=wp)

            def pepart(ch):
                G = GWBDS[ch % 2]
                wp = "ga" if ch % 2 == 0 else "gb"
                S.dma("pool", scr_wbd[ch], G.rearrange("p j r m -> p (j r m)"), r=[wp + "0", wp + "1"], w=["scrb%d" % ch])
                wv = G.rearrange("p j r m -> p (j r) m")
                for n_ in range(4):
                    def _tw(n_=n_, wv=wv, wp=wp):
                        pi = psget()
                        pb = PS[pi][:].bitcast(BF16)
                        tr_group([(pb[:, s_ * 128:(s_ + 1) * 128], wv[:, n_ * 8 + s_, :], IDB) for s_ in range(8)],
                                 r=[wp + "0", wp + "1", "idb"], w=["ps%d" % pi])
                        if n_ % 2 == 0:
                            act(GWIT[:, n_ * 8:(n_ + 1) * 8, :], pb.rearrange("p (s m) -> p s m", m=128), AF.Copy, r=["ps%d" % pi], w=["wit%d" % n_])
                        else:
                            cp(GWIT[:, n_ * 8:(n_ + 1) * 8, :], pb.rearrange("p (s m) -> p s m", m=128), r=["ps%d" % pi], w=["wit%d" % n_])
                    atomic(_tw)
                S.dma("pool", scr_wit[ch], GWIT.rearrange("p n m -> p (n m)"), r=["wit0", "wit1", "wit2", "wit3"], w=["scrw%d" % ch])

            elem(0)
            for ch in range(8):
                if ch + 1 < 8:
                    elem(ch + 1)
                pepart(ch)

        WITS = [WIT, WBD.rearrange("p j r m -> p (j r) m")]

        def phase_i(nck, XD):
            for ch in range(8):
                Wt = WITS[ch % 2]
                wk = "witb%d" % (ch % 2)
                S.dma("sp", Wt.rearrange("p n m -> p (n m)"), scr_wit[ch], r=["scrw%d" % ch], w=[wk])
                pis = [psget() for _ in range(4)]
                lst = []
                for ri in range(2):
                    for j in range(T):
                        for q in range(4):
                            lst.append((PS[pis[q]][:, ri * 256:ri * 256 + nck], Wt[32 * q:32 * q + 32, j * 2 + ri, :],
                                        UTP[32 * q:32 * q + 32, ch, j, 0:nck], j == 0, j == T - 1, (32 * q, 0)))
                mm_group(lst, r=[wk, "ut"], w=["ps%d" % p_ for p_ in pis])
                for q in range(4):
                    act(XD[:, :, ch * 4 + q, 0:nck], PS[pis[q]][:].rearrange("p (i k) -> p i k", k=256)[:, :, 0:nck], AF.Copy,
                        r=["ps%d" % pis[q]], w=["xh"])

        def scan_prefill(k_idx, xk_, TPl=TPA, kp="sa"):
            T3 = TPl[k_idx % len(TPl)]
            tk = "tp%s%d" % (kp, k_idx % len(TPl))
            o = T3[:, :, 2, :] if len(T3.shape) == 4 else T3[:, :, 2, :, :]
            act(o, xk_, AF.Copy, r=["xh"], w=[tk + "x"])

        def scan_step(cur, k_idx, coef=COEF, SAl=SA, TPl=TPA, bshape=[128, 2, 2, 32], kp="sa"):
            nxt = 1 - cur
            T3 = TPl[k_idx % len(TPl)]
            tk = "tp%s%d" % (kp, k_idx % len(TPl))
            if len(bshape) == 4:
                tt(T3[:, :, 0:2, :], coef, SAl[cur].unsqueeze(1).to_broadcast(bshape), ALU.mult, r=["coef", "%s%d" % (kp, cur)], w=[tk])
                rin = T3.rearrange("p o i r -> p o r i")
            else:
                tt(T3[:, :, 0:2, :, :], coef, SAl[cur].unsqueeze(1).to_broadcast(bshape), ALU.mult, r=["coef", "%s%d" % (kp, cur)], w=[tk])
                rin = T3.rearrange("p o i r s -> p o r s i")
            S.op("dve", lambda e, o_=SAl[nxt], rin=rin: e.tensor_reduce(out=o_, in_=rin, axis=mybir.AxisListType.X, op=ALU.add),
                 r=[tk, tk + "x"], w=["%s%d" % (kp, nxt)])
            return nxt

        xpre_v = xpre.rearrange("(b p) d -> p b d", p=128)
        for gi, (b0, b1) in enumerate(PRE_GROUPS):
            o = 4 * (gi % 2)
            if gi < 2:
                S.dma("sp", XRES[:, o:o + 4, :], xpre_v[:, b0:b1, :], w=["x%d" % (o + b) for b in range(4)])

        def pre_xb(g):
            return lambda b, o=4 * (g % 2): XRES[:, o + b, :]

        def pre_xk(g):
            return lambda b, o=4 * (g % 2): "x%d" % (o + b)

        def pre_tail(g):
            pass

        _orig_uproj = uproj

        def uproj_pre(nb, tok0, src, skey):
            _orig_uproj(nb, tok0, src, skey)
            gdone = tok0 // 512
            if gdone + 2 < len(PRE_GROUPS):
                b0_, b1_ = PRE_GROUPS[gdone + 2]
                o = 4 * (gdone % 2)
                S.dma("sp", XRES[:, o:o + 4, :], xpre_v[:, b0_:b1_, :], w=["x%d" % (o + b) for b in range(4)])

        uproj = uproj_pre
        load_gain(0, g_ffn1)
        load_gain(1, g_mix)
        for th in bg_dma:
            th()
        S.defer = bg
        gen_all()
        S.defer = None
        phase1(PRE_GROUPS, pre_xb, pre_xk, lambda g, h: (0 if (g, h) == (0, 0) else (8 if (g, h) in ((0, 1), (1, 0), (1, 1), (2, 0)) else 6)), lambda g: True, wc_of=lambda g: {"first": g == 0})
        uproj = _orig_uproj
        flush()
        XL = XN[(len(PRE_GROUPS) - 1) % 2]
        xlk = "xn%d" % ((len(PRE_GROUPS) - 1) % 2)
        for c in range(8):
            S.dma("pool", WGU[0][:, :, 0, :], colblock(w_in, c * 128), w=["wgu0g"])
            S.dma("pool", WGU[0][:, :, 1, :], colblock(w_in, 1024 + c * 128), w=["wgu0u"])
            pa, pb_ = psget(), psget()
            lo = 4 * 128 - 32
            mm_group([(PS[pa][:, 0:32], WGU[0][:, k, 0, :], XL[:, k, lo:lo + 32], k == 0, k == 7, None) for k in range(8)],
                     r=["wgu0g", xlk], w=["ps%d" % pa])
            mm_group([(PS[pb_][:, 0:32], WGU[0][:, k, 1, :], XL[:, k, lo:lo + 32], k == 0, k == 7, None) for k in range(8)],
                     r=["wgu0u", xlk], w=["ps%d" % pb_])
            act(SIL[0][:, 0:2], PS[pa][:, 30:32], AF.Copy, r=["ps%d" % pa], w=["sil0"])
            tt(ZPREV[:, c, :], SIL[0][:, 0:2], PS[pb_][:, 30:32], ALU.mult, r=["sil0", "ps%d" % pb_], w=["zprev"])
        S.barrier()
        phase_i(NCP, XHP)
        S.barrier()
        ms(SA[0].rearrange("p i r -> p (i r)"), 0.0, ["sa0"])
        S.defer = bg
        cur = 0
        scan_prefill(0, XHP[:, :, :, 0])
        for k in range(NCP):
            if k + 1 < NCP:
                scan_prefill(k + 1, XHP[:, :, :, k + 1])
            cur = scan_step(cur, k)
        S.defer = None
        S.defer_dma = None
        pre_cur = cur

        xown_v = xown.rearrange("(b p) d -> p b d", p=128)
        for gi, (b0, b1) in enumerate(OWN_GROUPS[:2]):
            S.dma("sp", XRES[:, b0:b1, :], xown_v[:, b0:b1, :], w=["x%d" % b for b in range(b0, b1)])

        def own_xb(g):
            return lambda b, b0=OWN_GROUPS[g][0]: XRES[:, b0 + b, :]

        def own_xk(g):
            return lambda b, b0=OWN_GROUPS[g][0]: "x%d" % (b0 + b)

        def own_tail(g):
            if g == 1:
                flush()
                for (b0, b1) in OWN_GROUPS[2:]:
                    S.dma("sp", XRES[:, b0:b1, :], xown_v[:, b0:b1, :], w=["x%d" % b for b in range(b0, b1)] + ["xh"])

        load_gain(0, g_ffn1)
        load_gain(1, g_mix)
        phase1(OWN_GROUPS, own_xb, own_xk, lambda g, h: (9 if g < 2 else 0), lambda g: g != 2, tail=own_tail, wc_of=lambda g: {"first": False})
        flush()
        S.barrier()
        phase_i(NCO, XH)
        S.barrier()
        SIN = W(22528, 2 * 128 * 4, F32, "p (i m) -> p i m", m=128)
        S.dma("sp", SIN[:, 0, :], sre, w=["sin"]); S.dma("sp", SIN[:, 1, :], sim, w=["sin"])
        pi = psget()
        tr_group([(PS[pi][:, i_ * 128:(i_ + 1) * 128], SIN[:, i_, :], IDF) for i_ in range(2)], r=["sin", "idf"], w=["ps%d" % pi])
        cp(S4[0], PS[pi][:, 0:256].rearrange("p (i s r) -> p i r s", i=2, s=4), r=["ps%d" % pi], w=["sb0"])
        cur = pre_cur
        scan_prefill(0, XH[:, :, :, 0])
        for k in range(128):
            act(HB[:, :, :, k], SA[cur], AF.Copy, r=["sa%d" % cur], w=["hb"])
            if k + 1 < 128:
                scan_prefill(k + 1, XH[:, :, :, k + 1])
            cur = scan_step(cur, k)
        cp(HF[:, 0, :, :], SA[cur], r=["sa%d" % cur], w=["hf"])
        coef4 = COEF.unsqueeze(4).to_broadcast([128, 2, 2, 32, 4])
        c4 = 0
        for st in range(2):
            act(HB[:, :, :, 128 + st:128 + st + 7:2], S4[c4], AF.Copy, r=["sb%d" % c4], w=["hb"])
            scan_prefill(st, XH[:, :, :, 128 + st:128 + st + 7:2], TPl=TP4A, kp="sb")
            c4 = scan_step(c4, st, coef=coef4, SAl=S4, TPl=TP4A, bshape=[128, 2, 2, 32, 4], kp="sb")
        for s_ in range(4):
            cp(HF[:, 1 + s_, :, :], S4[c4][:, :, :, s_], r=["sb%d" % c4], w=["hf"])
        HOUT = W(24576, 10 * 128 * 4, F32, "p (n m) -> p n m", m=128)
        S.barrier()
        for ri in range(2):
            n0 = 5 * ri
            pi = psget()
            tr_group([(PS[pi][0:32, s_ * 128:(s_ + 1) * 128], HF[:, s_, ri, :], IDF) for s_ in range(4)], r=["hf", "idf"], w=["ps%d" % pi])
            cp(HOUT[0:32, n0:n0 + 4, :], PS[pi][0:32, :].rearrange("p (s m) -> p s m", m=128), r=["ps%d" % pi], w=["hout"])
            pi = psget()
            tr_group([(PS[pi][0:32, 0:128], HF[:, 4, ri, :], IDF)], r=["hf", "idf"], w=["ps%d" % pi])
            cp(HOUT[0:32, n0 + 4, :], PS[pi][0:32, 0:128], r=["ps%d" % pi], w=["hout"])
        S.dma("sp", hre_o.rearrange("(s r) m -> r s m", r=32), HOUT[0:32, 0:5, :], r=["hout"])
        S.dma("sp", him_o.rearrange("(s r) m -> r s m", r=32), HOUT[0:32, 5:10, :], r=["hout"])
        S.barrier()

        KDS = [W(33792 + i * 4096, 4096, BF16, "p (d m) -> p d m", m=128) for i in range(2)]
        CLH = [W(46080, 4096, BF16, "p (q j i c) -> p q j i c", q=4, j=8, i=2),
               W(41984, 4096, BF16, "p (q j i c) -> p q j i c", q=4, j=8, i=2)]
        CTS = [W(50176 + i * 2048, 2048, F32, "p (q j c) -> p q j c", q=2, j=8) for i in range(4)]
        TAD = W(50176, 4096, F32); TBD = W(54272, 4096, F32)
        GT0 = W(58368, 512, F32); DG = W(58880, 512, F32)
        TMPU = W(59392, T * NCO * 2, BF16, "p (j k) -> p j k", j=T)
        WBD = WBD2

        def ssm_A(ch):
            KD = KDS[ch % 2]
            kk = "kd%d" % (ch % 2)
            S.dma("sp", WBD2.rearrange("p j r m -> p (j r m)"), scr_wbd[ch], r=["scrb%d" % ch], w=["wbd0", "wbd1"])
            kps = []
            for n_ in range(4):
                pi = psget(); kps.append(pi)
                lst = []
                for s_ in range(4):
                    d = n_ * 4 + s_
                    j = T - 1 - d
                    lst.append((PS[pi][:, s_ * 128:(s_ + 1) * 128], WBD[:, j, 0, :], CBr[:, ch * 4:(ch + 1) * 4, :].rearrange("p r c -> p (r c)"), True, False, None))
                    lst.append((PS[pi][:, s_ * 128:(s_ + 1) * 128], WBD[:, j, 1, :], CBni[:, ch * 4:(ch + 1) * 4, :].rearrange("p r c -> p (r c)"), False, True, None))
                mm_group(lst, r=["wbd0", "wbd1", "cbr", "cbni"], w=["ps%d" % pi])
            ts(DG, IDF, DV[:, ch:ch + 1], None, ALU.mult, None, r=["idf", "dv"], w=["dg"])
            for n_ in range(4):
                pi = kps[n_]
                psv = PS[pi][:].rearrange("p (s m) -> p s m", m=128)
                s0_ = 0
                if n_ == 0:
                    tt(GT0[:, 0:128], PS[pi][:, 0:128], MSK, ALU.mult, r=["ps%d" % pi, "msk"], w=["gt0"])
                    tt(KD[:, 0, :], GT0[:, 0:128], DG, ALU.add, r=["gt0", "dg"], w=[kk])
                    s0_ = 1
                for qq in range(4):
                    act(KD[:, n_ * 4 + s0_:(n_ + 1) * 4, 32 * qq:32 * qq + 32], psv[:, s0_:4, 32 * qq:32 * qq + 32], AF.Copy,
                        r=["ps%d" % pi, "msk"], w=[kk], scale=MSK[:, 32 * qq:32 * qq + 1])

        CTP = [W(25600 + i * 2048, 2048, F32, "p (q j c) -> p q j c", q=2, j=8) for i in range(4)]

        def ssm_cl(ch, hi):
            CLx = CLH[hi]
            ck = "cl%d" % hi
            j0 = 1 + 8 * hi
            for qh in range(2):
                on_dve = (hi == 1) or (qh == 1)
                eng = "dve" if on_dve else "pool"
                CTx = CTS if on_dve else CTP
                kx = ["ct0", "ct1", "ct2", "ct3"] if on_dve else ["cp0", "cp1", "cp2", "cp3"]
                p0_ = ch * 4 + qh * 2
                l1r = PWr[:, p0_:p0_ + 2, j0:j0 + 8].unsqueeze(3).to_broadcast([128, 2, 8, 32])
                l1i = PWi[:, p0_:p0_ + 2, j0:j0 + 8].unsqueeze(3).to_broadcast([128, 2, 8, 32])
                cr = CSr[:, p0_:p0_ + 2, :].unsqueeze(2).to_broadcast([128, 2, 8, 32])
                ci = CSi[:, p0_:p0_ + 2, :].unsqueeze(2).to_broadcast([128, 2, 8, 32])
                clh = CLx[:, qh * 2:qh * 2 + 2]
                tt(CTx[0], cr, l1r, ALU.mult, r=["csr", "pw"], w=[kx[0]], eng=eng)
                tt(CTx[1], ci, l1i, ALU.mult, r=["csi", "pw"], w=[kx[1]], eng=eng)
                tt(CTx[2], cr, l1i, ALU.mult, r=["csr", "pw"], w=[kx[2]], eng=eng)
                tt(CTx[3], ci, l1r, ALU.mult, r=["csi", "pw"], w=[kx[3]], eng=eng)
                tt(clh[:, :, :, 0, :], CTx[0], CTx[1], ALU.subtract, r=[kx[0], kx[1]], w=[ck], eng=eng)
                if eng == "dve":
                    stt(clh[:, :, :, 1, :], CTx[2], -1.0, CTx[3], ALU.mult, ALU.subtract, r=[kx[2], kx[3]], w=[ck], eng=eng)
                else:
                    tt(CTx[2], CTx[2], CTx[3], ALU.add, r=[kx[2], kx[3]], w=[kx[2]], eng=eng)
                    ts(clh[:, :, :, 1, :], CTx[2], -1.0, None, ALU.mult, None, r=[kx[2]], w=[ck], eng=eng)

        def ssm_Y(ch, j):
            KD = KDS[ch % 2]
            hi = j // 8
            CLx = CLH[hi]
            pi = psget()
            lst = []
            for d in range(j + 1):
                lst.append((PS[pi][:, 0:NCO], KD[:, d, :], UTP[:, ch, j - d, :], d == 0, False, None))
            for ri in range(2):
                for q in range(4):
                    lst.append((PS[pi][32 * q:32 * q + 32, 0:NCO], CLx[:, q, j % 8, ri, :], HB[:, ri, ch * 4 + q, :], False, ri == 1, (0, 32 * q)))
            mm_group(lst, r=["kd%d" % (ch % 2), "cl%d" % hi, "hb"] + ["ut%d_%d" % (ch, jj_) for jj_ in range(j + 1)], w=["ps%d" % pi])
            act(UTP[:, ch, j, :], PS[pi][:, 0:NCO], AF.Gelu_apprx_tanh, r=["ps%d" % pi], w=["ut%d_%d" % (ch, j)])

        ssm_A(0); ssm_cl(0, 1); ssm_cl(0, 0)
        for ch in range(8):
            for j in range(15, 7, -1):
                ssm_Y(ch, j)
            if ch + 1 < 8:
                ssm_cl(ch + 1, 1)
            for j in range(7, 4, -1):
                ssm_Y(ch, j)
            if ch + 1 < 8:
                ssm_A(ch + 1)
            for j in range(4, -1, -1):
                ssm_Y(ch, j)
            if ch + 1 < 8:
                ssm_cl(ch + 1, 0)
            uk = ["ut%d_%d" % (ch, jj_) for jj_ in range(T)]
            act(TMPU, UTP[:, ch, :, :], AF.Copy, r=uk, w=["tmpu"])
            act(UT[:, ch, :].rearrange("p (k j) -> p j k", j=T), TMPU, AF.Copy, r=["tmpu"], w=uk)
        S.barrier()

        HT0 = W(0, 8 * 640 * 2, BF16, "p (c t) -> p c t", t=640)
        ZC0 = W(10240, 8 * 640 * 2, BF16, "p (c t) -> p c t", t=640)
        HT1 = W(10240, 8 * 512 * 2, BF16, "p (c t) -> p c t", t=512)
        ZC1 = W(0, 8 * 512 * 2, BF16, "p (c t) -> p c t", t=512)
        MRG = W(20480, 8 * 640 * 2, BF16, "p (c t) -> p c t", t=640)
        NWS = 7
        WS = [W(30720 + i * 2048, 2048, BF16, "p (k m) -> p k m", m=128) for i in range(NWS)]
        WOBS = [W(45056 + i * 4096, 8 * 256 * 2, BF16, "p (k m) -> p k m", m=256) for i in range(2)]
        ZB = W(53248, 2816, F32); VS = W(56064, 2048, F32); OB = W(58112, 2816, F32)
        SGC = W(53248, 2048, F32); M1 = W(55296, 2048, F32); SB_ = W(57344, 2048, F32); SGS = W(59392, 2048, F32); TT_ = W(61440, 2048, F32)
        SCI = W(53248, 4096, F32)
        S.dma("sp", SCI[0:8, :], sconv, w=["sci"])
        pi = psget()
        tr_group([(PS[pi][:, c * 8:(c + 1) * 8], SCI[0:8, c * 128:(c + 1) * 128], IDF[0:8, 0:8]) for c in range(8)], r=["sci", "idf"], w=["ps%d" % pi])
        cp(SCT, PS[pi][:, 0:64].rearrange("p (c k) -> p c k", k=8), r=["ps%d" % pi], w=["sct"])
        ms(CST.rearrange("p c k -> p (c k)"), 0.0, ["cst"])
        S.barrier()
        y_v = y_o.rearrange("(b p) d -> p b d", p=128)
        load_gain(1, g_mix)

        def final_norm(gi_):
            b0_, b1_ = OWN_GROUPS[gi_]
            nb_ = b1_ - b0_
            xb_ = lambda b, b0_=b0_: XRES[:, b0_ + b, :]
            xk_ = lambda b, b0_=b0_: "x%d" % (b0_ + b)
            load_gain(0, g_fin)
            norm_stats(xb_, nb_, xk_)
            for b in range(nb_):
                stt(xb_(b), xb_(b), RSTD[:, b:b + 1], GBC[:, 0, :], ALU.mult, ALU.mult, r=[xk_(b), "rstd", "gbc0"], w=[xk_(b)])
                S.dma("sp", y_v[:, b0_ + b, :], xb_(b), r=[xk_(b)])

        for gi, (b0, b1) in enumerate(OWN_GROUPS):
            nb = b1 - b0
            ntok = nb * 128
            tok0 = b0 * 128
            xb = lambda b, b0=b0: XRES[:, b0 + b, :]
            xk = lambda b, b0=b0: "x%d" % (b0 + b)
            last = gi == len(OWN_GROUPS) - 1
            if gi == 0:
                HTg, ZCg, htk = HT0, ZC0, "xnt"
                norm_T(xb, nb, 1, HTg, htk, xk)
            else:
                HTg, ZCg, htk = HT1, ZC1, "xn1"
                S.defer = bg
                final_norm(gi - 1)
                S.defer = None
            npr = ntok - 128 if last else ntok
            segs = [(0, npr, "p")] + ([(npr + 32 * s_, 32, s_) for s_ in range(4)] if last else [])
            tls = tiles_of(nb)
            for c in range(8):
                sl3 = [(3 * c + i_) % NWS for i_ in range(3)]
                for i_, base in enumerate((0, 1024, 2048)):
                    S.dma("pool", WS[sl3[i_]], colblock(w_in, base + c * 128), w=["ws%d" % sl3[i_]])
                pump(3)
                for (c0, c1) in tls:
                    n = c1 - c0
                    pa, pb_, pc = psget(), psget(), psget()
                    for pp, sl in ((pa, sl3[0]), (pb_, sl3[1]), (pc, sl3[2])):
                        mm_group([(PS[pp][:, 0:n], WS[sl][:, k, :], HTg[:, k, c0:c1], k == 0, k == 7, None) for k in range(8)],
                                 r=["ws%d" % sl, htk], w=["ps%d" % pp])
                    act(VS[:, 0:n], PS[pa][:, 0:n], AF.Copy, r=["ps%d" % pa], w=["vs"])
                    sg = [(s0, ln, kd) for (s0, ln, kd) in segs if s0 < c1 and s0 + ln > c0]
                    sg = [(i_, s0, ln, kd) for i_, (s0, ln, kd) in enumerate(sg)]
                    lo_pad = None
                    for (i_, s0, ln, kd) in sg:
                        a0, a1 = max(s0, c0), min(s0 + ln, c1)
                        zoff = 2 * (i_ + 1) + a0 - c0 if True else 0
                        if a0 == s0:
                            if kd == "p":
                                cp(ZB[:, zoff - 2:zoff], ZPREV[:, c, :], r=["zprev"], w=["zb"])
                            else:
                                cp(ZB[:, zoff - 2:zoff], SCT[:, c, 2 * kd:2 * kd + 2], r=["sct"], w=["zb"])
                        else:
                            cp(ZB[:, zoff - 2:zoff], ZPREV[:, c, :], r=["zprev"], w=["zb"])
                        tt(ZB[:, zoff:zoff + a1 - a0], VS[:, a0 - c0:a1 - c0], PS[pb_][:, a0 - c0:a1 - c0], ALU.mult, r=["vs", "ps%d" % pb_], w=["zb"])
                        if kd == "p":
                            cp(ZPREV[:, c, :], ZB[:, zoff + a1 - a0 - 2:zoff + a1 - a0], r=["zb"], w=["zprev"])
                            if a1 == s0 + ln and last:
                                cp(CST[:, c, 0:2], ZB[:, zoff + a1 - a0 - 2:zoff + a1 - a0], r=["zb"], w=["cst"])
                        else:
                            cp(CST[:, c, 2 + 2 * kd:4 + 2 * kd], ZB[:, zoff + a1 - a0 - 2:zoff + a1 - a0], r=["zb"], w=["cst"])
                    i0 = sg[0][0]
                    zlo = 2 * (i0 + 1) + max(sg[0][1], c0) - c0
                    zhi = 2 * (sg[-1][0] + 1) + min(sg[-1][1] + sg[-1][2], c1) - c0
                    L = zhi - zlo
                    ts(OB[:, 0:L], ZB[:, zlo:zhi], WC[:, 2, c:c + 1], None, ALU.mult, None, r=["zb", "wc"], w=["ob"])
                    stt(OB[:, 0:L], ZB[:, zlo - 1:zhi - 1], WC[:, 1, c:c + 1], OB[:, 0:L], ALU.mult, ALU.add, r=["zb", "wc", "ob"], w=["ob"])
                    stt(OB[:, 0:L], ZB[:, zlo - 2:zhi - 2], WC[:, 0, c:c + 1], OB[:, 0:L], ALU.mult, ALU.add, r=["zb", "wc", "ob"], w=["ob"])
                    for (i_, s0, ln, kd) in sg:
                        a0, a1 = max(s0, c0), min(s0 + ln, c1)
                        zoff = 2 * (i_ + 1) + a0 - c0
                        tt(ZCg[:, c, a0:a1], OB[:, zoff - zlo:zoff - zlo + a1 - a0], PS[pc][:, a0 - c0:a1 - c0], ALU.mult, r=["ob", "ps%d" % pc], w=["zc"])
            flush()
            S.barrier()
            for c in range(8):
                sl5 = [(5 * c + i_) % NWS for i_ in range(5)]
                srcs = [(w_co, c * 128), (w_in, 4096 + c * 128), (w_glu, c * 128), (w_glu, 1024 + c * 128), (w_in, 5120 + c * 128)]
                for i_, (wd_, c0_) in enumerate(srcs):
                    S.dma("pool", WS[sl5[i_]], colblock(wd_, c0_), w=["ws%d" % sl5[i_]])
                for (c0, c1) in tls:
                    n = c1 - c0
                    pa, pb_ = psget(), psget()
                    mm_group([(PS[pa][:, 0:n], WS[sl5[0]][:, k, :], ZCg[:, k, c0:c1], k == 0, k == 7, None) for k in range(8)], r=["ws%d" % sl5[0], "zc"], w=["ps%d" % pa])
                    mm_group([(PS[pb_][:, 0:n], WS[sl5[1]][:, k, :], HTg[:, k, c0:c1], k == 0, k == 7, None) for k in range(8)], r=["ws%d" % sl5[1], htk], w=["ps%d" % pb_])
                    act(SGC[:, 0:n], PS[pb_][:, 0:n], AF.Sigmoid, r=["ps%d" % pb_], w=["sgc"])
                    tt(M1[:, 0:n], SGC[:, 0:n], PS[pa][:, 0:n], ALU.mult, r=["sgc", "ps%d" % pa], w=["m1"])
                    pc, pd, pe_ = psget(), psget(), psget()
                    mm_group([(PS[pc][:, 0:n], WS[sl5[2]][:, k, :], UT[:, k, tok0 + c0:tok0 + c1], k == 0, k == 7, None) for k in range(8)], r=["ws%d" % sl5[2], "ut"], w=["ps%d" % pc])
                    mm_group([(PS[pd][:, 0:n], WS[sl5[3]][:, k, :], UT[:, k, tok0 + c0:tok0 + c1], k == 0, k == 7, None) for k in range(8)], r=["ws%d" % sl5[3], "ut"], w=["ps%d" % pd])
                    mm_group([(PS[pe_][:, 0:n], WS[sl5[4]][:, k, :], HTg[:, k, c0:c1], k == 0, k == 7, None) for k in range(8)], r=["ws%d" % sl5[4], htk], w=["ps%d" % pe_])
                    act(SB_[:, 0:n], PS[pd][:, 0:n], AF.Sigmoid, r=["ps%d" % pd], w=["sb"])
                    act(SGS[:, 0:n], PS[pe_][:, 0:n], AF.Sigmoid, r=["ps%d" % pe_], w=["sgs"])
                    tt(TT_[:, 0:n], SB_[:, 0:n], PS[pc][:, 0:n], ALU.mult, r=["sb", "ps%d" % pc], w=["tt"])
                    tt(TT_[:, 0:n], TT_[:, 0:n], SGS[:, 0:n], ALU.mult, r=["tt", "sgs"], w=["tt"])
                    tt(MRG[:, c, c0:c1], TT_[:, 0:n], M1[:, 0:n], ALU.add, r=["tt", "m1"], w=["mrg"])
            for qd in range(4):
                WOB = WOBS[qd % 2]
                S.dma("pool", WOB, w_o[:, qd * 256:(qd + 1) * 256].rearrange("(k p) m -> p k m", p=128), w=["wob%d" % (qd % 2)])
                for b in range(nb):
                    pi = psget()
                    mm_group([(PS[pi][:, 0:256], MRG[:, k, b * 128:(b + 1) * 128], WOB[:, k, :], k == 0, k == 7, None) for k in range(8)],
                             r=["mrg", "wob%d" % (qd % 2)], w=["ps%d" % pi])
                    xs = xb(b)[:, qd * 256:(qd + 1) * 256]
                    tt(xs, xs, PS[pi][:, 0:256], ALU.add, r=["ps%d" % pi, xk(b)], w=[xk(b)])
            S.barrier()
            load_gain(0, g_ffn2)
            norm_T(xb, nb, 0, XNT, "xnt", xk)
            for half in range(2):
                hooks = None
                if half == 1 and not last:
                    nb0_, nb1_ = OWN_GROUPS[gi + 1]
                    nbn = nb1_ - nb0_
                    xbn = lambda b, b0_=nb0_: XRES[:, b0_ + b, :]
                    xkn = lambda b, b0_=nb0_: "x%d" % (b0_ + b)
                    hooks = {1: (lambda xbn=xbn, xkn=xkn, nbn=nbn: norm_stats(xbn, nbn, xkn, SS2, RSTD2, "ss2", "rstd2")),
                             6: (lambda xbn=xbn, xkn=xkn, nbn=nbn: norm_apply(xbn, nbn, 1, HT1, "xn1", xkn, RSTD2, "rstd2"))}
                ffn_gu(half, XNT, "xnt", nb, w2g, w2u, w2d, 0, hooks)
                ffn_dn(half, xb, nb, xk, w2d)
            if last:
                final_norm(gi)
            S.barrier()
        COUT = W(0, 4096, F32)
        pi = psget()
        tr_group([(PS[pi][0:10, c * 128:(c + 1) * 128], CST[:, c, :], IDF) for c in range(4)], r=["cst", "idf"], w=["ps%d" % pi])
        cp(COUT[0:10, 0:512], PS[pi][0:10, :], r=["ps%d" % pi], w=["cout"])
        pi = psget()
        tr_group([(PS[pi][0:10, c * 128:(c + 1) * 128], CST[:, 4 + c, :], IDF) for c in range(4)], r=["cst", "idf"], w=["ps%d" % pi])
        cp(COUT[0:10, 512:1024], PS[pi][0:10, :], r=["ps%d" % pi], w=["cout"])
        S.dma("sp", conv_o, COUT[0:10, :], r=["cout"])
        S.barrier()
        S.emit(block)
    return nc


_NC_CACHE = {}


def kernel(**inp):
    f32 = np.float32
    a = {k: np.asarray(v, dtype=f32) for k, v in inp.items()}
    if "nc" not in _NC_CACHE:
        _NC_CACHE["nc"] = build_program()
    nc = _NC_CACHE["nc"]
    ident = np.eye(128, dtype=f32)
    qmask = np.kron(np.eye(4, dtype=f32), np.ones((32, 32), dtype=f32))
    shared = {
        "norm_ffn1": a["norm_ffn1"].reshape(1, D), "norm_mix": a["norm_mix"].reshape(1, D),
        "norm_ffn2": a["norm_ffn2"].reshape(1, D), "norm_final": a["norm_final"].reshape(1, D),
        "w_ffn1_gate": a["w_ffn1_gate"][0], "w_ffn1_up": a["w_ffn1_up"][0], "w_ffn1_down": a["w_ffn1_down"][0],
        "w_ffn2_gate": a["w_ffn2_gate"][0], "w_ffn2_up": a["w_ffn2_up"][0], "w_ffn2_down": a["w_ffn2_down"][0],
        "w_in": a["w_in"][0], "w_conv": a["w_conv"][0], "w_conv_out": a["w_conv_out"][0],
        "ssm_lambda_re": a["ssm_lambda_re"][0], "ssm_lambda_im": a["ssm_lambda_im"][0], "ssm_log_step": a["ssm_log_step"].reshape(1, 64),
        "ssm_b_re": a["ssm_b_re"][0], "ssm_b_im": a["ssm_b_im"][0], "ssm_c_re": a["ssm_c_re"][0], "ssm_c_im": a["ssm_c_im"][0],
        "ssm_d": a["ssm_d"].reshape(1, D), "w_glu": a["w_glu"][0], "w_o": a["w_o"][0],
        "ident": ident, "qmask": qmask,
    }
    shared = {k: np.ascontiguousarray(v) for k, v in shared.items()}
    xp, xs = a["x_prompt"], a["x_sample"]
    in_maps = []
    for c in range(8):
        b, h = c // 2, c % 2
        m = dict(shared)
        m["xown"] = np.ascontiguousarray(np.concatenate([xp[b, h * 2048:(h + 1) * 2048], xs[4 * c:4 * c + 4].reshape(128, D)], axis=0))
        m["xpre"] = np.ascontiguousarray(xp[b, 0:2048]) if h == 1 else np.zeros((2048, D), f32)
        m["sconv"] = np.ascontiguousarray(a["state_conv"][0, 4 * c:4 * c + 4].reshape(8, D))
        m["sre"] = np.ascontiguousarray(a["state_ssm_re"][0, 4 * c:4 * c + 4].reshape(128, 128))
        m["sim"] = np.ascontiguousarray(a["state_ssm_im"][0, 4 * c:4 * c + 4].reshape(128, 128))
        in_maps.append(m)
    res = run_bass_kernel_spmd(nc, in_maps, core_ids=list(range(8)))
    R = res.results
    y_prompt = np.zeros((4, 4096, D), f32); y_sample = np.zeros((32, 32, D), f32)
    conv_p = np.zeros((1, 4, 2, D), f32); conv_s = np.zeros((1, 32, 2, D), f32)
    rp = np.zeros((1, 4, 64, 64), f32); ip = np.zeros((1, 4, 64, 64), f32)
    rs = np.zeros((1, 32, 64, 64), f32); is_ = np.zeros((1, 32, 64, 64), f32)
    for c in range(8):
        b, h = c // 2, c % 2
        y = R[c]["y"]
        y_prompt[b, h * 2048:(h + 1) * 2048] = y[0:2048]
        y_sample[4 * c:4 * c + 4] = y[2048:].reshape(4, 32, D)
        co = R[c]["conv_o"].reshape(5, 2, D)
        hr = R[c]["hre_o"].reshape(5, 64, 64); hi = R[c]["him_o"].reshape(5, 64, 64)
        conv_s[0, 4 * c:4 * c + 4] = co[1:5]
        rs[0, 4 * c:4 * c + 4] = hr[1:5]; is_[0, 4 * c:4 * c + 4] = hi[1:5]
        if h == 1:
            conv_p[0, b] = co[0]; rp[0, b] = hr[0]; ip[0, b] = hi[0]
    return (y_prompt, y_sample, conv_p, rp, ip, conv_s, rs, is_)
```
